# Optimizing a Trainium2 kernel written in Bass

```python
import jax, jax.numpy as jnp
from jax import lax
import numpy as np

D_MODEL = 1024
BATCH = 16
SEQ = 2048
DEPTH = 1

N_META = 16
EPS = 1e-6
SSD_HEADS = 16
SSD_HEAD_DIM = 64
SSD_INNER = SSD_HEADS * SSD_HEAD_DIM
SSD_GROUPS = 4
SSD_HPG = SSD_HEADS // SSD_GROUPS
SSD_STATE = 128
SSD_CONV = 4
SSD_CHUNK = 128
SSD_CONV_CH = SSD_INNER + 2 * SSD_GROUPS * SSD_STATE
HG_WIDTH = 1024
HG_EXPAND = 128
HG_HEADS = HG_WIDTH // HG_EXPAND
HG_HEAD_I = HG_WIDTH // HG_HEADS
HG_CHUNK = 16
D_FF = 2816
IN_SIZES = (SSD_INNER, SSD_CONV_CH, SSD_HEADS, HG_WIDTH, HG_WIDTH, HG_WIDTH, HG_WIDTH, D_MODEL, D_MODEL)
IN_TOTAL = sum(IN_SIZES)

kernel_name = "hybrid_ssd_hgrn2_macaron_block"


def _split_points():
    pts, acc = [], 0
    for s in IN_SIZES[:-1]:
        acc += s
        pts.append(acc)
    return pts


def rmsnorm(x, w):
    xf = x.astype(jnp.float32)
    y = xf * lax.rsqrt(jnp.mean(xf * xf, axis=-1, keepdims=True) + EPS)
    return (y * w.astype(jnp.float32)).astype(x.dtype)


def swiglu(x, w_gu, w_down):
    g, u = jnp.split(x @ w_gu, 2, axis=-1)
    return (jax.nn.silu(g) * u) @ w_down


def causal_depthwise_conv(x, w, b):
    y = lax.conv_general_dilated(x, w[:, None, :], window_strides=(1,), padding=[(w.shape[0] - 1, 0)],
                                 dimension_numbers=("NWC", "WIO", "NWC"), feature_group_count=x.shape[-1])
    return y + b


def segsum_exp(a):
    T = a.shape[-1]
    cs = jnp.cumsum(a, axis=-1)
    mask = jnp.tril(jnp.ones((T, T), dtype=bool))
    return jnp.exp(jnp.where(mask, cs[..., :, None] - cs[..., None, :], -jnp.inf))


def ssd_mixer(z, xbc, dt_raw, conv_w, conv_b, dt_bias, a_log, d_skip, norm_w):
    f32 = jnp.float32
    Bsz, L, _ = z.shape
    G, R, P, N, Q = SSD_GROUPS, SSD_HPG, SSD_HEAD_DIM, SSD_STATE, SSD_CHUNK
    xbc = jax.nn.silu(causal_depthwise_conv(xbc, conv_w, conv_b)).astype(f32)
    xs, Bm, Cm = jnp.split(xbc, [SSD_INNER, SSD_INNER + G * N], axis=-1)
    dt = jax.nn.softplus(dt_raw.astype(f32) + dt_bias.astype(f32))
    A = -jnp.exp(a_log.astype(f32))
    pad = (-L) % Q
    padf = lambda t: jnp.pad(t, ((0, 0), (pad, 0)) + ((0, 0),) * (t.ndim - 2))
    Lp = L + pad
    nc = Lp // Q
    x4 = padf(xs).reshape(Bsz, nc, Q, G, R, P)
    dtp = padf(dt).reshape(Bsz, nc, Q, SSD_HEADS)
    Bc = padf(Bm).reshape(Bsz, nc, Q, G, N)
    Cc = padf(Cm).reshape(Bsz, nc, Q, G, N)
    X = x4 * dtp.reshape(Bsz, nc, Q, G, R)[..., None]
    a = (dtp * A).transpose(0, 3, 1, 2)
    a_cs = jnp.cumsum(a, axis=-1)
    Lmat = segsum_exp(a).reshape(Bsz, G, R, nc, Q, Q)
    CB = jnp.einsum("bclgn,bcsgn->bcgls", Cc, Bc)
    y_diag = jnp.einsum("bcgls,bgrcls,bcsgrp->bclgrp", CB, Lmat, X)
    decay_states = jnp.exp(a_cs[..., -1:] - a_cs).reshape(Bsz, G, R, nc, Q)
    states = jnp.einsum("bclgn,bgrcl,bclgrp->cbgrpn", Bc, decay_states, X)
    chunk_decay = jnp.moveaxis(jnp.exp(a_cs[..., -1]).reshape(Bsz, G, R, nc), -1, 0)

    def step(hs, inp):
        s, dec = inp
        return hs * dec[..., None, None] + s, hs

    _, prev = lax.scan(step, jnp.zeros((Bsz, G, R, P, N), f32), (states, chunk_decay))
    y_off = jnp.einsum("bclgn,cbgrpn,bgrcl->bclgrp", Cc, prev, jnp.exp(a_cs).reshape(Bsz, G, R, nc, Q))
    y = y_diag + y_off + x4 * d_skip.astype(f32).reshape(G, R)[:, :, None]
    y = y.reshape(Bsz, Lp, SSD_INNER)[:, pad:]
    yg = (y * jax.nn.silu(z.astype(f32))).reshape(Bsz, L, G, SSD_INNER // G)
    yg = yg * lax.rsqrt(jnp.mean(yg * yg, axis=-1, keepdims=True) + EPS)
    return (yg.reshape(Bsz, L, SSD_INNER) * norm_w.astype(f32)).astype(z.dtype)


def hgrn2_mixer(q, f_logit, i_in, g_out, lb, norm_w):
    f32 = jnp.float32
    Bsz, L, _ = q.shape
    H, K, V, C = HG_HEADS, HG_EXPAND, HG_HEAD_I, HG_CHUNK
    nc = L // C
    f = lb + (1.0 - lb) * jax.nn.sigmoid(f_logit.astype(f32))
    chunked = lambda t, d: jnp.moveaxis(t.reshape(Bsz, nc, C, H, d), 1, 0)
    qs = chunked(jax.nn.silu(q.astype(f32)), K)
    ks = chunked(1.0 - f, K)
    vs = chunked(i_in.astype(f32), V)
    gs = chunked(jnp.log(f), K)
    tri = jnp.tril(jnp.ones((C, C), dtype=bool))[None, :, :, None, None]

    def step(S, inp):
        qc, kc, vc, gc = inp
        Gc = jnp.cumsum(gc, axis=1)
        o_inter = jnp.einsum("blhk,bhkv->blhv", qc * jnp.exp(Gc), S)
        dec = jnp.exp(jnp.where(tri, Gc[:, :, None] - Gc[:, None, :], -jnp.inf))
        att = jnp.einsum("blhk,bshk,blshk->bhls", qc, kc, dec)
        o = o_inter + jnp.einsum("bhls,bshv->blhv", att, vc)
        G_last = Gc[:, -1]
        S_new = jnp.exp(G_last)[..., None] * S + jnp.einsum(
            "bshk,bshv->bhkv", kc * jnp.exp(G_last[:, None] - Gc), vc)
        return S_new, o

    _, o = lax.scan(step, jnp.zeros((Bsz, H, K, V), f32), (qs, ks, vs, gs))
    o = jnp.moveaxis(o, 0, 1).reshape(Bsz, L, H, V)
    o = o * lax.rsqrt(jnp.mean(o * o, axis=-1, keepdims=True) + EPS) * norm_w.astype(f32).reshape(H, V)
    o = o.reshape(Bsz, L, HG_WIDTH) * jax.nn.silu(g_out.astype(f32))
    return o.astype(q.dtype)


def setup_inputs(seed: int = 0) -> dict:
    key = jax.random.key(seed)
    ks = jax.random.split(key, 24)
    nrm = lambda k, shape, s: jax.random.normal(k, shape, jnp.float32) * s
    gain = lambda k, shape: 1.0 + 0.02 * jax.random.normal(k, shape, jnp.float32)
    dt0 = jnp.exp(jax.random.uniform(ks[9], (DEPTH, SSD_HEADS), jnp.float32, np.log(1e-3), np.log(1e-1)))
    return {
        "x": nrm(ks[0], (BATCH, SEQ, D_MODEL), 1.0),
        "meta_tokens": nrm(ks[1], (N_META, D_MODEL), 1.0),
        "ffn1_norm": gain(ks[2], (DEPTH, D_MODEL)),
        "ffn1_w_gu": nrm(ks[3], (DEPTH, D_MODEL, 2 * D_FF), D_MODEL ** -0.5),
        "ffn1_w_down": nrm(ks[4], (DEPTH, D_FF, D_MODEL), D_FF ** -0.5),
        "mix_norm": gain(ks[5], (DEPTH, D_MODEL)),
        "w_in": nrm(ks[6], (DEPTH, D_MODEL, IN_TOTAL), D_MODEL ** -0.5),
        "ssd_conv_w": nrm(ks[7], (DEPTH, SSD_CONV, SSD_CONV_CH), SSD_CONV ** -0.5),
        "ssd_conv_b": nrm(ks[8], (DEPTH, SSD_CONV_CH), 0.01),
        "ssd_dt_bias": dt0 + jnp.log(-jnp.expm1(-dt0)),
        "ssd_a_log": jnp.log(jax.random.uniform(ks[10], (DEPTH, SSD_HEADS), jnp.float32, 1.0, 16.0)),
        "ssd_d": 1.0 + 0.1 * jax.random.normal(ks[11], (DEPTH, SSD_HEADS), jnp.float32),
        "ssd_norm": gain(ks[12], (DEPTH, SSD_INNER)),
        "hg_lower_bound": 1.0 + 0.1 * jax.random.normal(ks[13], (DEPTH + 1, HG_WIDTH), jnp.float32),
        "hg_norm": gain(ks[14], (DEPTH, HG_WIDTH)),
        "w_branch_a": nrm(ks[15], (DEPTH, SSD_INNER, D_MODEL), SSD_INNER ** -0.5),
        "w_branch_b": nrm(ks[16], (DEPTH, HG_WIDTH, D_MODEL), HG_WIDTH ** -0.5),
        "w_out": nrm(ks[17], (DEPTH, D_MODEL, D_MODEL), D_MODEL ** -0.5),
        "ffn2_norm": gain(ks[18], (DEPTH, D_MODEL)),
        "ffn2_w_gu": nrm(ks[19], (DEPTH, D_MODEL, 2 * D_FF), D_MODEL ** -0.5),
        "ffn2_w_down": nrm(ks[20], (DEPTH, D_FF, D_MODEL), D_FF ** -0.5),
        "final_norm": gain(ks[21], (D_MODEL,)),
    }


def reference(x, meta_tokens, ffn1_norm, ffn1_w_gu, ffn1_w_down, mix_norm, w_in, ssd_conv_w, ssd_conv_b,
              ssd_dt_bias, ssd_a_log, ssd_d, ssd_norm, hg_lower_bound, hg_norm, w_branch_a, w_branch_b,
              w_out, ffn2_norm, ffn2_w_gu, ffn2_w_down, final_norm):
    Bsz = x.shape[0]
    meta = jnp.broadcast_to(meta_tokens[None].astype(x.dtype), (Bsz, N_META, D_MODEL))
    h = jnp.concatenate([meta, x], axis=1)
    lb_all = jnp.cumsum(jax.nn.softmax(hg_lower_bound.astype(jnp.float32), axis=0), axis=0)
    splits = _split_points()
    for l in range(DEPTH):
        h = h + 0.5 * swiglu(rmsnorm(h, ffn1_norm[l]), ffn1_w_gu[l], ffn1_w_down[l])
        u = rmsnorm(h, mix_norm[l])
        z, xbc, dt_raw, q, f_logit, i_in, g_out, gate_a, gate_b = jnp.split(u @ w_in[l], splits, axis=-1)
        y_a = ssd_mixer(z, xbc, dt_raw, ssd_conv_w[l], ssd_conv_b[l], ssd_dt_bias[l], ssd_a_log[l],
                        ssd_d[l], ssd_norm[l])
        y_b = hgrn2_mixer(q, f_logit, i_in, g_out, lb_all[l], hg_norm[l])
        merged = jax.nn.sigmoid(gate_a) * (y_a @ w_branch_a[l]) + jax.nn.sigmoid(gate_b) * (y_b @ w_branch_b[l])
        h = h + merged @ w_out[l]
        h = h + 0.5 * swiglu(rmsnorm(h, ffn2_norm[l]), ffn2_w_gu[l], ffn2_w_down[l])
    h = rmsnorm(h, final_norm)
    return h[:, N_META:]
```

```python
import numpy as np
import concourse.bass as bass
import concourse.mybir as mybir
from concourse.bass_utils import run_bass_kernel_spmd

F32 = mybir.dt.float32
BF16 = mybir.dt.bfloat16
AF = mybir.ActivationFunctionType
ALU = mybir.AluOpType

D = 1024
SEQ = 2048
NMETA = 16
DFF = 2816
NJ = 22
EPS = 1e-6
W_IN_COLS = 9232
SAME_ENGINE_SYNC = True


class Prog:
    ENGS = ("pe", "act", "dve", "pool", "sp")
    EPOCH = 3000

    def __init__(self, nc, nd):
        self.nc = nc
        self.ops = []
        self.eops = {e: [] for e in self.ENGS}
        self.last_w = {}
        self.rd_eng = {}
        self.rd_dma = {}
        self.seen = {e: {f: -1 for f in self.ENGS} for e in self.ENGS}
        self.dma_seen = {e: set() for e in self.ENGS}
        self.dma_ops = {e: [] for e in self.ENGS}
        self.ND = nd

    def op(self, eng, fn, reads=(), writes=(), dma=False):
        idx = len(self.ops)
        o = {"eng": eng, "fn": fn, "dma": dma, "idx": idx, "eidx": len(self.eops[eng]),
             "sig": False, "waits": [], "tag": getattr(self, "tag", "")}
        raw = set()
        oth = set()
        psr = [k for k in reads if isinstance(k, tuple) and k[0] == "ps"]
        if psr:
            writes = list(writes) + [k for k in psr if k not in writes]
        for k in reads:
            w = self.last_w.get(k)
            if w is not None:
                raw.add(w)
        for k in writes:
            w = self.last_w.get(k)
            if w is not None:
                oth.add(w)
            r = self.rd_eng.get(k)
            if r:
                oth.update(r.values())
            r = self.rd_dma.get(k)
            if r:
                oth.update(r)
        if dma:
            i = len(self.dma_ops[eng])
            o["dma_i"] = i
            if i >= self.ND[eng]:
                oth.add(self.dma_ops[eng][i - self.ND[eng]])
            self.dma_ops[eng].append(idx)
        best = {}
        for d in raw | oth:
            y = self.ops[d]
            if y["dma"]:
                if d in self.dma_seen[eng]:
                    continue
                self.dma_seen[eng].add(d)
                y["sig"] = True
                o["waits"].append(d)
            else:
                f = y["eng"]
                if f == eng and not dma:
                    if eng == "pe" or not SAME_ENGINE_SYNC:
                        continue
                if self.seen[eng][f] >= y["eidx"]:
                    continue
                if f not in best or self.ops[best[f]]["eidx"] < y["eidx"]:
                    best[f] = d
        for f, d in best.items():
            y = self.ops[d]
            self.seen[eng][f] = y["eidx"]
            y["sig"] = True
            o["waits"].append(d)
        for k in reads:
            if dma:
                self.rd_dma.setdefault(k, []).append(idx)
            else:
                self.rd_eng.setdefault(k, {})[eng] = idx
        for k in writes:
            self.last_w[k] = idx
            self.rd_eng[k] = {}
            self.rd_dma[k] = []
        self.ops.append(o)
        self.eops[eng].append(idx)
        return idx

    def emit(self):
        nc = self.nc
        dsem = {}
        for e in self.ENGS:
            if self.dma_ops[e]:
                dsem[e] = [nc.alloc_semaphore("d%s%d" % (e, i)) for i in range(self.ND[e])]
        csem = {}
        for e in self.ENGS:
            cnt = 0
            for idx in self.eops[e]:
                o = self.ops[idx]
                if o["dma"]:
                    i = o["dma_i"]
                    o["sem"] = dsem[e][i % self.ND[e]]
                    o["val"] = 16 * (i // self.ND[e] + 1)
                elif o["sig"]:
                    ep = cnt // self.EPOCH
                    if (e, ep) not in csem:
                        csem[(e, ep)] = nc.alloc_semaphore("c%s%d" % (e, ep))
                    o["sem"] = csem[(e, ep)]
                    o["val"] = cnt % self.EPOCH + 1
                    cnt += 1
        ops = self.ops

        def run(e, engine):
            for idx in self.eops[e]:
                o = ops[idx]
                for d in o["waits"]:
                    y = ops[d]
                    engine.wait_ge(y["sem"], y["val"])
                ins = o["fn"](engine)
                if o["dma"]:
                    ins.then_inc(o["sem"], 16)
                elif o["sig"]:
                    ins.then_inc(o["sem"], 1)
            if e == "sp":
                for ee in self.ENGS:
                    n = len(self.dma_ops[ee])
                    if n == 0:
                        continue
                    nd = self.ND[ee]
                    for j in range(min(nd, n)):
                        cntj = (n - 1 - j) // nd + 1
                        engine.wait_ge(dsem[ee][j], 16 * cntj)

        with nc.Block() as block:
            @block.sync
            def _(eng):
                run("sp", eng)

            @block.tensor
            def _(eng):
                run("pe", eng)

            @block.scalar
            def _(eng):
                run("act", eng)

            @block.vector
            def _(eng):
                run("dve", eng)

            @block.gpsimd
            def _(eng):
                run("pool", eng)


def build_nc(nseq=2, ntile=4, debug=None, stop=None):
    nc = bass.Bass("TRN2", target_bir_lowering=False)
    P = Prog(nc, {"sp": 40, "pool": 12, "act": 4, "dve": 4, "pe": 4})
    TT = 512

    def din(name, shape):
        return nc.dram_tensor(name, list(shape), F32, kind="ExternalInput").ap()

    x_d = din("x", [nseq, SEQ, D])
    meta_d = din("meta_tokens", [NMETA, D])
    ffn1_norm_d = din("ffn1_norm", [D])
    ffn1_gu_d = din("ffn1_w_gu", [D, 2 * DFF])
    ffn1_dn_d = din("ffn1_w_down", [DFF, D])
    mix_norm_d = din("mix_norm", [D])
    w_in_d = din("w_in", [D, W_IN_COLS])
    conv_w_d = din("ssd_conv_w", [4, 2048])
    conv_b_d = din("ssd_conv_b", [2048])
    dt_bias_d = din("ssd_dt_bias", [16])
    a_log_d = din("ssd_a_log", [16])
    dskip_d = din("ssd_d", [16])
    ssd_norm_d = din("ssd_norm", [D])
    hg_lb_d = din("hg_lower_bound", [2, D])
    hg_norm_d = din("hg_norm", [D])
    wa_d = din("w_branch_a", [D, D])
    wb_d = din("w_branch_b", [D, D])
    wo_d = din("w_out", [D, D])
    ffn2_norm_d = din("ffn2_norm", [D])
    ffn2_gu_d = din("ffn2_w_gu", [D, 2 * DFF])
    ffn2_dn_d = din("ffn2_w_down", [DFF, D])
    fnorm_d = din("final_norm", [D])
    out_d = nc.dram_tensor("out", [nseq, SEQ, D], F32, kind="ExternalOutput").ap()

    def dscr(name, shape):
        return nc.dram_tensor(name, list(shape), BF16, kind="Internal").ap()

    scr = {
        "gu1": dscr("s_gu1", [11, 128, 4096]), "gu2": dscr("s_gu2", [11, 128, 4096]),
        "win": dscr("s_win", [18, 128, 4096]),
        "wa": dscr("s_wa", [2, 128, 4096]), "wb": dscr("s_wb", [2, 128, 4096]),
        "wo": dscr("s_wo", [2, 128, 4096]),
        "dn1": dscr("s_dn1", [NJ, 128, 1024]), "dn2": dscr("s_dn2", [NJ, 128, 1024]),
    }

    def sb(name, shape, dt=F32):
        return nc.alloc_sbuf_tensor(name, list(shape), dt)

    identf = sb("identf", [128, 128]); identb = sb("identb", [128, 128], BF16)
    trif = sb("trif", [128, 128]); slf = sb("slf", [128, 128]); onesf = sb("onesf", [128, 128])
    fnorm_bc = sb("fnorm_bc", [128, D])
    PR = sb("PR", [8, 2048]); NR = sb("NR", [8, D])
    PT = sb("PT", [128, 16, 8]); NT = sb("NT", [128, 8, 8])
    lbp = sb("lbp", [128, 8, 3])
    dtb_bc = sb("dtb_bc", [128, 16]); A_bc = sb("A_bc", [128, 16]); D_bc = sb("D_bc", [128, 16])
    wdt = sb("wdt", [128, 8, 16], BF16)
    h = sb("h", [128, 4, D])
    xnT = sb("xnT", [128, 8, TT], BF16)
    xn_tm = sb("xn_tm", [128, D], BF16)
    junk = sb("junk", [128, D], BF16)
    ss = sb("ss", [128, 64]); rs = sb("rs", [128, 64])
    actT = sb("actT", [128, 6, TT], BF16)
    wd = sb("wd", [128, 6, D], BF16)
    NSLOT = 3
    ring = [sb("ring%d" % i, [128, 8, 512], BF16) for i in range(NSLOT)]
    xsT = sb("xsT", [128, 8, TT], BF16)
    BT = sb("BT", [128, 4, TT], BF16); CT = sb("CT", [128, 4, TT], BF16)
    zs = sb("zs", [128, 4, D], BF16); gs = sb("gs", [128, 4, D], BF16)
    qT = sb("qT", [128, 8, TT], BF16); kT = sb("kT", [128, 8, TT], BF16)
    vtm = sb("vtm", [128, 4, D], BF16)
    xpad = sb("xpad", [128, TT + 3]); cacc = sb("cacc", [128, TT])
    ctail = sb("ctail", [128, 16, 3]); m_ctail = sb("m_ctail", [128, 16, 3])
    t1 = sb("t1", [128, TT]); t2 = sb("t2", [128, TT]); t3 = sb("t3", [128, TT])
    gsc = sb("gsc", [128, 8, 4, 4])
    dtt = sb("dtt", [128, 4, 16]); att_a = sb("att_a", [128, 4, 16]); dtmp = sb("dtmp", [128, 16])
    ecs = sb("ecs", [128, 16]); cdb = sb("cdb", [128, 16])
    R = sb("R", [128, 8, 128]); L = sb("L", [128, 8, 128])
    Mt = sb("Mt", [128, 16, 128], BF16)
    CBm = sb("CBm", [128, 4, 128])
    X = sb("X", [128, D], BF16); Xd = sb("Xd", [128, D], BF16); xstm = sb("xstm", [128, D], BF16)
    Btm = sb("Btm", [128, 512], BF16)
    y1 = sb("y1", [128, D]); y2 = sb("y2", [128, D]); y3 = sb("y3", [128, D])
    gr = sb("gr", [128, 8, 4, 2])
    attm = sb("attm", [128, 8, 128], BF16); ktm = sb("ktm", [128, D], BF16)
    Sbf = sb("Sbf", [128, D], BF16)
    ssdS = sb("ssdS", [128, D]); ssdSbf = sb("ssdSbf", [128, D], BF16); hgS = sb("hgS", [128, D])
    m_ssdS = nc.dram_tensor("m_ssdS", [128, D], F32, kind="Internal").ap()
    m_hgS = nc.dram_tensor("m_hgS", [128, D], F32, kind="Internal").ap()
    ytm = xn_tm
    sg = t1
    wdf = wd[:, :, :].rearrange("p j f -> p (j f)").bitcast(F32)
    xpad2 = wdf[:, 0:TT + 3]; cacc2 = wdf[:, 516:516 + TT]
    t1b = wdf[:, 1028:1028 + TT]; t2b = wdf[:, 1540:1540 + TT]; t3b = wdf[:, 2052:2052 + TT]
    dummy = sb("fdummy", [128, 2])
    xpadb = xpad[:, :].bitcast(BF16)
    xpbs = [xpadb[:, 0:TT + 3], xpadb[:, TT + 3:2 * (TT + 3)]]
    caccb = cacc[:, :].bitcast(BF16)
    dgs = [[caccb[:, (k * 4 + j) * 128:(k * 4 + j + 1) * 128] for j in range(4)] for k in range(2)]
    ALIAS_KEYS = ["xpad2", "cacc2", "t1b", "t2b", "t3b"]

    def fence(reads, writes):
        P.op("dve", lambda e: e.memset(dummy[:, 0:1], 0.0), list(reads), list(writes) + ["dummy"])
    ps = [nc.alloc_psum_tensor("ps%d" % i, [128, 512], F32) for i in range(8)]

    def PS(b):
        return ("ps", b)

    class StopBuild(Exception):
        pass

    def ck(name):
        if stop == name and rstate.get("main"):
            raise StopBuild()

    def act(out, in_, func, reads, writes, **kw):
        P.op("act", lambda e: e.activation(out, in_, func, **kw), reads, writes)

    def tt(eng, out, a, b, op, reads, writes):
        P.op(eng, lambda e: e.tensor_tensor(out, a, b, op), reads, writes)

    def ts(eng, out, a, s1, s2, op0, op1, reads, writes):
        if s2 is None:
            P.op(eng, lambda e: e.tensor_scalar(out, a, s1, None, op0), reads, writes)
        else:
            P.op(eng, lambda e: e.tensor_scalar(out, a, s1, s2, op0, op1), reads, writes)

    def stt(out, a, s, b, op0, op1, reads, writes):
        P.op("dve", lambda e: e.scalar_tensor_tensor(out, a, s, b, op0, op1), reads, writes)

    def cp(eng, out, in_, reads, writes):
        if eng == "act":
            P.op("act", lambda e: e.copy(out, in_), reads, writes)
        else:
            P.op(eng, lambda e: e.tensor_copy(out, in_), reads, writes)

    def mm(out, lhsT, rhs, start, stop, reads, writes):
        i = P.op("pe", lambda e: e.matmul(out, lhsT, rhs, start=start, stop=stop), reads, writes)
        P.ops[i]["f32"] = (lhsT.dtype == F32)

    def dma(eng, out, in_, reads, writes):
        P.op(eng, lambda e: e.dma_start(out=out, in_=in_), reads, writes, dma=True)

    eps_t = sb("eps_t", [128, 1])
    one_t = sb("one_t", [128, 1])
    P.op("pool", lambda e: e.memset(eps_t[:, :], EPS), (), ["eps_t"])
    P.op("pool", lambda e: e.memset(one_t[:, :], 1.0), (), ["one_t"])
    P.op("pool", lambda e: e.memset(onesf[:, :], 1.0), (), ["onesf"])
    for tname, t, pat, cm, cmpop in (("identf", identf, [[-1, 128]], 1, ALU.is_equal),
                                     ("trif", trif, [[1, 128]], -1, ALU.is_ge),
                                     ("slf", slf, [[-1, 128]], 1, ALU.is_gt)):
        P.op("pool", lambda e, t=t, pat=pat, cm=cm, cmpop=cmpop: e.affine_select(
            t[:, :], onesf[:, :], pat, cmpop, 0.0, base=0, channel_multiplier=cm),
            ["onesf"], [tname])
    cp("pool", identb[:, :], identf[:, :], ["identf"], ["identb"])
    P.op("pool", lambda e: e.memset(PR[:, :], 0.0), (), ["PR"])
    P.op("pool", lambda e: e.memset(NR[:, :], 0.0), (), ["NR"])
    P.op("pool", lambda e: e.memset(ssdS[:, :], 0.0), (), ["ssdS"])
    P.op("pool", lambda e: e.memset(ssdSbf[:, :], 0.0), (), ["ssdSbf"])
    P.op("pool", lambda e: e.memset(hgS[:, :], 0.0), (), ["hgS"])
    P.op("pool", lambda e: e.memset(ctail[:, :, :], 0.0), (), ["ctail"])
    dma("pool", PR[0:4, :], conv_w_d[:, :], ["PR"], ["PR"])
    dma("pool", PR[4:5, :], conv_b_d[None, :], ["PR"], ["PR"])
    dma("pool", PR[5:7, 0:D], hg_lb_d[:, :], ["PR"], ["PR"])
    for i, nd_ in enumerate((ffn1_norm_d, mix_norm_d, ffn2_norm_d, ssd_norm_d, hg_norm_d)):
        dma("pool", NR[i:i + 1, :], nd_[None, :], ["NR"], ["NR"])
    dma("pool", fnorm_bc[:, :], fnorm_d.partition_broadcast(128), (), ["fnorm_bc"])
    dma("pool", dtb_bc[:, :], dt_bias_d.partition_broadcast(128), (), ["dtb_bc"])
    dma("pool", A_bc[:, :], a_log_d.partition_broadcast(128), (), ["A_bc"])
    dma("pool", D_bc[:, :], dskip_d.partition_broadcast(128), (), ["D_bc"])
    dma("pool", wdt[:, :, :], w_in_d[:, 3072:3088].rearrange("(kc p) c -> p kc c", p=128), (), ["wdt"])
    def cast_gu(name, src):
        for i in range(11):
            dst = scr[name][i].rearrange("p (kc c) -> p kc c", kc=8)
            dma("pool", dst[:, :, 0:256],
                src[:, 256 * i:256 * i + 256].rearrange("(kc p) c -> p kc c", p=128), (), [("scrh", name, i, 0)])
            dma("pool", dst[:, :, 256:512],
                src[:, DFF + 256 * i:DFF + 256 * i + 256].rearrange("(kc p) c -> p kc c", p=128), (),
                [("scrh", name, i, 1)])

    def cast_dn(name, src):
        for g0 in range(0, NJ, 6):
            g1 = min(NJ, g0 + 6)
            dma("pool", scr[name][g0:g1].rearrange("j p f -> p j f"),
                src[g0 * 128:g1 * 128, :].rearrange("(j p) f -> p j f", p=128), (), [("scr", name, g0)])

    def cast_sq(name, src, cols, order=None):
        for i in (order if order is not None else range(len(cols))):
            c0 = cols[i]
            dma("pool", scr[name][i].rearrange("p (kc c) -> p kc c", kc=8),
                src[:, c0:c0 + 512].rearrange("(kc p) c -> p kc c", p=128), (), [("scr", name, i)])

    win_cols = [1024, 1536, 2048, 2560, 3088, 3600, 4112, 4624, 5136, 5648, 0, 512, 6160, 6672,
                7184, 7696, 8208, 8720]
    cast_gu("gu1", ffn1_gu_d)
    cast_dn("dn1", ffn1_dn_d)
    cast_sq("win", w_in_d, win_cols, order=[6, 7, 0, 1, 2, 3, 8, 9, 4, 5, 10, 11, 12, 13, 14, 15, 16, 17])
    cast_sq("wa", wa_d, [0, 512]); cast_sq("wb", wb_d, [0, 512]); cast_sq("wo", wo_d, [0, 512])
    cast_gu("gu2", ffn2_gu_d)
    cast_dn("dn2", ffn2_dn_d)

    if stop == 0:
        P.emit(); return nc
    for ch in range(16):
        mm(ps[0][:, ch * 8:ch * 8 + 8], PR[0:8, ch * 128:(ch + 1) * 128], identf[0:8, 0:8], True, True,
           ["PR", "identf"], [PS(0)])
    cp("dve", PT[:, :, :], ps[0][:, 0:128].rearrange("p (c r) -> p c r", r=8), [PS(0)], ["PT"])
    for ch in range(8):
        mm(ps[1][:, ch * 8:ch * 8 + 8], NR[0:8, ch * 128:(ch + 1) * 128], identf[0:8, 0:8], True, True,
           ["NR", "identf"], [PS(1)])
    cp("dve", NT[:, :, :], ps[1][:, 0:64].rearrange("p (c r) -> p c r", r=8), [PS(1)], ["NT"])
    tt("dve", lbp[:, :, 2], PT[:, 0:8, 5], PT[:, 0:8, 6], ALU.subtract, ["PT"], ["lbp"])
    act(lbp[:, :, 0], lbp[:, :, 2], AF.Sigmoid, ["lbp"], ["lbp"])
    ts("dve", lbp[:, :, 1], lbp[:, :, 0], -1.0, 1.0, ALU.mult, ALU.add, ["lbp"], ["lbp"])
    ts("dve", lbp[:, :, 2], lbp[:, :, 1], -1.0, None, ALU.mult, None, ["lbp"], ["lbp"])
    act(A_bc[:, :], A_bc[:, :], AF.Exp, ["A_bc"], ["A_bc"])
    ts("dve", A_bc[:, :], A_bc[:, :], -1.0, None, ALU.mult, None, ["A_bc"], ["A_bc"])

    if stop == 1:
        P.emit(); return nc
    def mixer_sched(meta):
        if meta:
            return [("f", 0), ("f", 1), ("conv", 0), ("conv", 1), ("conv", 2), ("conv", 3), ("i", 0), ("i", 1)]
        return [("f", 0), ("f", 1), ("conv", 0), ("conv", 1), ("conv", 2), ("conv", 3), ("q", 0), ("q", 1),
                ("i", 0), ("i", 1), ("z", 0), ("z", 1), ("g", 0), ("g", 1)]

    def tile_panels(meta):
        seq = [("gu1", i) for i in range(11)]
        wbase = {"conv": 0, "q": 4, "f": 6, "i": 8, "z": 10, "g": 12}
        seq += [("win", wbase[k] + a) for (k, a) in mixer_sched(meta)]
        if not meta:
            seq += [("win", 14), ("win", 15), ("win", 16), ("win", 17)]
            seq += [("wa", 0), ("wb", 0), ("wa", 1), ("wb", 1), ("wo", 0), ("wo", 1)]
            seq += [("gu2", i) for i in range(11)]
        return seq

    plan = tile_panels(True)
    for _ in range(nseq * ntile):
        plan += tile_panels(False)
    rstate = {"next_load": 0, "next_get": 0}

    def ring_load():
        i = rstate["next_load"]
        if i >= len(plan):
            return
        name, pi = plan[i]
        slot = ring[i % NSLOT]
        rk = [("scrh", name, pi, 0), ("scrh", name, pi, 1)] if name.startswith("gu") else [("scr", name, pi)]
        dma("sp", slot[:, :, :], scr[name][pi].rearrange("p (kc c) -> p kc c", kc=8), rk, [("ring", i % NSLOT)])
        rstate["next_load"] = i + 1

    for _ in range(NSLOT):
        ring_load()

    def ring_get(name, pi):
        i = rstate["next_get"]
        assert plan[i] == (name, pi), (plan[i], name, pi)
        rstate["next_get"] = i + 1
        return ring[i % NSLOT], ("ring", i % NSLOT)

    def ring_release():
        ring_load()

    ncall = {"n": 0}

    def norm_T(subs, wrow, dstT, dkey):
        P.tag = 'norm'
        for (s, off, n) in subs:
            c = ncall["n"] % 64
            ncall["n"] += 1
            act(junk[:n, :], h[:n, s, :], AF.Square, [("h", s)], ["junk", ("ss", c)], accum_out=ss[:n, c:c + 1])
            act(rs[:n, c:c + 1], ss[:n, c:c + 1], AF.Sqrt, [("ss", c), "eps_t"], [("rs", c)], scale=1.0 / D, bias=eps_t[:n, 0:1])
            P.op("dve", lambda e, c=c, n=n: e.reciprocal(rs[:n, c:c + 1], rs[:n, c:c + 1]), [("rs", c)], [("rs", c)])
            ts("dve", xn_tm[:n, :], h[:n, s, :], rs[:n, c:c + 1], None, ALU.mult, None,
               [("h", s), ("rs", c)], ["xn_tm"])
            transpose_tm(xn_tm, n, "xn_tm", dstT, dkey, s, off, wrow)

    def transpose_tm(src, n, skey, dstT, dkey, s, off, wrow, banks=(6, 7)):
        for half in range(2):
            b = banks[half]
            for cc in range(4):
                c = half * 4 + cc
                mm(ps[b][:, cc * 128:cc * 128 + n], src[:n, c * 128:(c + 1) * 128], identb[:n, :n], True, True,
                   [skey, "identb"], [PS(b)])
            tt("dve", dstT[:, half * 4:half * 4 + 4, off:off + n],
               ps[b][:, :].rearrange("p (c t) -> p c t", c=4)[:, :, 0:n],
               NT[:, half * 4:half * 4 + 4, wrow:wrow + 1].broadcast_to([128, 4, n]), ALU.mult,
               [PS(b), "NT"], [(dkey, s)])


    dcnt = {"n": 0}

    def ffn(subs, T, gname, dname, wrow, prenormed=False):
        if not prenormed:
            norm_T(subs, wrow, xnT, "xnT")
        P.tag = 'ffn_gu'
        xk = [("xnT", s) for (s, _, _) in subs]
        groups = [(0, 3), (3, 6), (6, 9), (9, 11)]
        cnt = 0
        for (p0, p1) in groups:
            j0 = 2 * p0
            nj = 2 * (p1 - p0)
            P.tag = 'ffn_gu'
            dma("sp", wd[:, 0:nj, :], scr[dname][j0:j0 + nj].rearrange("j p f -> p j f"),
                [("scr", dname, (j0 // 6) * 6)], ["wd"])
            for pi in range(p0, p1):
                slot, skey = ring_get(gname, pi)
                for jj in range(2):
                    jl = 2 * (pi - p0) + jj
                    bg = cnt % 2
                    bu = 2 + cnt % 2
                    cnt += 1
                    for kc in range(8):
                        mm(ps[bg][:, 0:T], slot[:, kc, jj * 128:(jj + 1) * 128], xnT[:, kc, 0:T], kc == 0, kc == 7,
                           [skey] + xk, [PS(bg)])
                    for kc in range(8):
                        mm(ps[bu][:, 0:T], slot[:, kc, 256 + jj * 128:256 + (jj + 1) * 128], xnT[:, kc, 0:T],
                           kc == 0, kc == 7, [skey] + xk, [PS(bu)])
                    act(sg[:, 0:T], ps[bg][:, 0:T], AF.Silu, [PS(bg)], ["t1"])
                    tt("dve", actT[:, jl, 0:T], ps[bu][:, 0:T], sg[:, 0:T], ALU.mult, [PS(bu), "t1"], [("actT", jl)])
                ring_release()
            ak = [("actT", jl) for jl in range(nj)]
            P.tag = 'ffn_dn'
            for (s, off, n) in subs:
                for hf in range(2):
                    b = 4 + dcnt["n"] % 4
                    dcnt["n"] += 1
                    for jl in range(nj):
                        mm(ps[b][:n, :], actT[:, jl, off:off + n], wd[:, jl, hf * 512:(hf + 1) * 512],
                           jl == 0, jl == nj - 1, ak + ["wd"], [PS(b)])
                    stt(h[:n, s, hf * 512:(hf + 1) * 512], ps[b][:n, :], 0.5, h[:n, s, hf * 512:(hf + 1) * 512],
                        ALU.mult, ALU.add, [PS(b), ("h", s)], [("h", s)])

    def featmajor_chunk(slot, skey, cc, T, uk, b):
        for kc in range(8):
            mm(ps[b][:, 0:T], slot[:, kc, cc * 128:(cc + 1) * 128], xnT[:, kc, 0:T], kc == 0, kc == 7,
               [skey] + uk, [PS(b)])

    def mixer(subs, T, meta):
        NS = len(subs)
        norm_T(subs, 1, xnT, "xnT")
        uk = [("xnT", s) for (s, _, _) in subs]
        sbe = "dve"
        if not meta:
            fence(["wd"] + [("actT", jl) for jl in range(6)], ALIAS_KEYS)
        P.tag = 'dt'
        for (s, off, n) in subs:
            bd = 4 + s % 4
            for kc in range(8):
                mm(ps[bd][:n, 0:16], xnT[:, kc, off:off + n], wdt[:, kc, :], kc == 0, kc == 7, uk + ["wdt"], [PS(bd)])
            tt("dve", dtmp[:n, :], ps[bd][:n, 0:16], dtb_bc[:n, :], ALU.add, [PS(bd), "dtb_bc"], ["dtmp"])
            act(dtmp[:n, :], dtmp[:n, :], AF.Exp, ["dtmp"], ["dtmp"])
            act(dtt[:n, s, :], dtmp[:n, :], AF.Ln, ["dtmp", "one_t"], [("dtt", s)], bias=one_t[:n, 0:1])
            tt("dve", att_a[:n, s, :], dtt[:n, s, :], A_bc[:n, :], ALU.mult, [("dtt", s), "A_bc"], [("att_a", s)])
        bcs = {"bc": 0, "tmb": 0}

        pend = {"p": None}

        def conv_flush():
            if pend["p"] is not None:
                pend["p"]()
                pend["p"] = None

        def conv_panel(pi):
            P.tag = 'conv'
            slot, skey = ring_get("win", pi)
            for cc in range(4):
                ch = pi * 4 + cc
                b = bcs["bc"] % 4
                bcs["bc"] += 1
                if ch < 8:
                    dst = xsT[:, ch, 0:T]; dk = [("xsT", s) for (s, _, _) in subs]
                elif ch < 12:
                    dst = BT[:, ch - 8, 0:T]; dk = [("BT", s) for (s, _, _) in subs]
                else:
                    dst = CT[:, ch - 12, 0:T]; dk = [("CT", s) for (s, _, _) in subs]
                featmajor_chunk(slot, skey, cc, T, uk, b)
                if meta:
                    xp, ca, xpk, cak = xpad, cacc, "xpad", "cacc"
                    cp(sbe, xp[:, 0:3], ctail[:, ch, :], [("ctail", ch)], [xpk])
                    cp("act", xp[:, 3:3 + T], ps[b][:, 0:T], [PS(b), xpk], [xpk])
                    cp(sbe, ctail[:, ch, :], xp[:, T:T + 3], [xpk], [("ctail", ch)])
                    ts("dve", ca[:, 0:T], xp[:, 0:T], PT[:, ch, 0:1], PT[:, ch, 4:5], ALU.mult, ALU.add,
                       [xpk, "PT"], [cak])
                    for k in range(1, 4):
                        stt(ca[:, 0:T], xp[:, k:k + T], PT[:, ch, k:k + 1], ca[:, 0:T], ALU.mult, ALU.add,
                            [xpk, cak, "PT"], [cak])
                    act(dst, ca[:, 0:T], AF.Silu, [cak], dk)
                    continue
                k2 = ch % 2
                xpb = xpbs[k2]
                xpk = "xpb%d" % k2
                dgk = ("dg", k2)
                cp("dve", xpb[:, 0:3], ctail[:, ch, :], [("ctail", ch)], [xpk])
                cp("act", xpb[:, 3:3 + T], ps[b][:, 0:T], [PS(b), xpk], [xpk])
                cp("dve", ctail[:, ch, :], xpb[:, T:T + 3], [xpk], [("ctail", ch)])
                for j in range(4):
                    ts("dve", dgs[k2][j], identb[:, :], PT[:, ch, j:j + 1], None, ALU.mult, None,
                       ["identb", "PT"], [dgk])
                conv_flush()

                def stage_b(ch=ch, k2=k2, xpb=xpb, xpk=xpk, dgk=dgk, dst=dst, dk=dk):
                    b2 = 4 + ch % 2
                    for j in range(4):
                        mm(ps[b2][:, 0:T], dgs[k2][j], xpb[:, j:j + T], j == 0, j == 3, [dgk, xpk], [PS(b2)])
                    act(dst, ps[b2][:, 0:T], AF.Silu, [PS(b2), "PT"], dk, bias=PT[:, ch, 4:5])
                pend["p"] = stage_b
            ring_release()
            if pi == 3:
                conv_flush()

        def q_panel(pi):
            P.tag = 'q'
            slot, skey = ring_get("win", 4 + pi)
            for cc in range(4):
                hd = pi * 4 + cc
                b = bcs["bc"] % 4
                bcs["bc"] += 1
                tq, tqk = (t1, "t1") if hd % 2 == 0 else (t1b, "t1b")
                featmajor_chunk(slot, skey, cc, T, uk, b)
                act(tq[:, 0:T], ps[b][:, 0:T], AF.Silu, [PS(b)], [tqk])
                qk = [("qT", s) for (s, _, _) in subs]
                tt("dve", qT[:, hd, 0:T], tq[:, 0:T], qT[:, hd, 0:T], ALU.mult, [tqk] + qk, qk)
            ring_release()

        def f_head(hd, slot, skey, cc, tset):
            (a, ak), (b2, bk), (c3, ck_) = tset
            b = bcs["bc"] % 4
            bcs["bc"] += 1
            n0 = subs[0][2]
            featmajor_chunk(slot, skey, cc, T, uk, b)
            act(a[:, 0:T], ps[b][:, 0:T], AF.Sigmoid, [PS(b)], [ak])
            yield
            ts("dve", b2[:, 0:T], a[:, 0:T], lbp[:, hd, 1:2], lbp[:, hd, 0:1], ALU.mult, ALU.add,
               [ak, "lbp"], [bk])
            yield
            act(c3[:, 0:T], b2[:, 0:T], AF.Ln, [bk], [ck_])
            ts("dve", a[:, 0:T], a[:, 0:T], lbp[:, hd, 2:3], lbp[:, hd, 1:2], ALU.mult, ALU.add,
               [ak, "lbp"], [ak])
            yield
            for (s, off, n) in subs:
                P.op("dve", lambda e, off=off, n=n: e.tensor_tensor_scan(
                    b2[:, off:off + n], onesf[:, 0:n], c3[:, off:off + n], 0.0, ALU.mult, ALU.add),
                    [ck_, "onesf"], [bk])
            b2v = b2[:, 0:T].rearrange("p (s t) -> p s t", t=n0)
            c3v = c3[:, 0:T].rearrange("p (s t) -> p s t", t=n0)
            cp("dve", gr[:, hd, 0:NS, 0:1], b2v[:, :, n0 // 2 - 1:n0 // 2], [bk], [("gr", hd)])
            cp("dve", gr[:, hd, 0:NS, 1:2], b2v[:, :, n0 - 1:n0], [bk], [("gr", hd)])
            tt("dve", c3v, b2v, gr[:, hd, 0:NS, 0:1].broadcast_to([128, NS, n0]), ALU.subtract,
               [bk, ("gr", hd)], [ck_])
            yield
            act(b2[:, 0:T], c3[:, 0:T], AF.Exp, [ck_], [bk], scale=-1.0)
            if not meta:
                act(qT[:, hd, 0:T], c3[:, 0:T], AF.Exp, [ck_], [("qT", s) for (s, _, _) in subs])
            yield
            tt("dve", kT[:, hd, 0:T], a[:, 0:T], b2[:, 0:T], ALU.mult, [ak, bk],
               [("kT", s) for (s, _, _) in subs])
            yield

        def f_panel(pi):
            P.tag = 'f'
            slot, skey = ring_get("win", 6 + pi)
            setA = ((t1, "t1"), (t2, "t2"), (t3, "t3"))
            setB = setA if meta else ((t1b, "t1b"), (t2b, "t2b"), (t3b, "t3b"))
            for pair in range(2):
                gens = [f_head(pi * 4 + pair * 2 + k, slot, skey, pair * 2 + k, (setA, setB)[k]) for k in range(2)]
                if meta:
                    for g_ in gens:
                        for _ in g_:
                            pass
                else:
                    alive = [True, True]
                    while any(alive):
                        for gi in range(2):
                            if alive[gi]:
                                try:
                                    next(gens[gi])
                                except StopIteration:
                                    alive[gi] = False
            ring_release()
            if pi == 1:
                grk = [("gr", hd) for hd in range(8)]
                tt("dve", gsc[:, :, 0:NS, 3], gr[:, :, 0:NS, 1], gr[:, :, 0:NS, 0], ALU.subtract, grk, ["gsc"])
                act(gsc[:, :, 0:NS, 0], gr[:, :, 0:NS, 0], AF.Exp, grk, ["gsc"])
                act(gsc[:, :, 0:NS, 1], gr[:, :, 0:NS, 1], AF.Exp, grk, ["gsc"])
                act(gsc[:, :, 0:NS, 2], gsc[:, :, 0:NS, 3], AF.Exp, ["gsc"], ["gsc"])

        def tm_panel(pname, pi):
            P.tag = 'izg'
            dstt, fn = {"i": (vtm, None), "z": (zs, AF.Silu), "g": (gs, AF.Silu)}[pname]
            base = {"i": 8, "z": 10, "g": 12}[pname]
            dkn = {"i": "vtm", "z": "zs", "g": "gs"}[pname]
            slot, skey = ring_get("win", base + pi)
            for (s, off, n) in subs:
                b = 4 + bcs["tmb"] % 2
                bcs["tmb"] += 1
                for kc in range(8):
                    mm(ps[b][:n, :], xnT[:, kc, off:off + n], slot[:, kc, :], kc == 0, kc == 7,
                       [skey] + uk, [PS(b)])
                if fn is None:
                    cp("act", dstt[:n, s, pi * 512:(pi + 1) * 512], ps[b][:n, :], [PS(b)], [(dkn, s)])
                else:
                    act(dstt[:n, s, pi * 512:(pi + 1) * 512], ps[b][:n, :], fn, [PS(b)], [(dkn, s)])
            ring_release()

        for (kind, arg) in mixer_sched(meta):
            if kind == "conv":
                conv_panel(arg)
            elif kind == "q":
                q_panel(arg)
            elif kind == "f":
                f_panel(arg)
            else:
                tm_panel(kind, arg)
        if not meta:
            fence(ALIAS_KEYS, ["wd"])
        bc = bcs["bc"]
        tmb = bcs["tmb"]
        ck("m5")
        for (s, off, n) in subs:
            gens = [ssd_chunk(s, off, n, meta, sbe), hgrn_chunk(s, off, n, meta, sbe)]
            tags = ['ssd', 'hgrn']
            alive = [True, True]
            while any(alive):
                for gi in range(2):
                    if alive[gi]:
                        P.tag = tags[gi]
                        try:
                            next(gens[gi])
                        except StopIteration:
                            alive[gi] = False
        if meta:
            return
        P.tag = 'gates'
        zsv = zs[:, :, :].rearrange("p s f -> p (s f)").rearrange("p (c t) -> p c t", c=8)
        gsv = gs[:, :, :].rearrange("p s f -> p (s f)").rearrange("p (c t) -> p c t", c=8)
        allz = [("zs", s) for (s, _, _) in subs]
        allg = [("gs", s) for (s, _, _) in subs]
        for (gi, view, keys) in ((0, zsv, allz), (1, gsv, allg)):
            for pi in range(2):
                slot, skey = ring_get("win", 14 + 2 * gi + pi)
                for cc in range(4):
                    b = bc % 4
                    bc += 1
                    featmajor_chunk(slot, skey, cc, T, uk, b)
                    act(view[:, pi * 4 + cc, 0:T], ps[b][:, 0:T], AF.Sigmoid, [PS(b)], keys)
                ring_release()
        ck("m8")
        P.tag = 'branch'
        yak = [("xsT", s) for (s, _, _) in subs]
        ybk = [("qT", s) for (s, _, _) in subs]
        mk = [("kT", s) for (s, _, _) in subs]
        for pi in range(2):
            sa, ska = ring_get("wa", pi)
            sb_, skb = ring_get("wb", pi)
            for cc in range(4):
                c = pi * 4 + cc
                for kc in range(8):
                    mm(ps[0][:, 0:T], sa[:, kc, cc * 128:(cc + 1) * 128], xsT[:, kc, 0:T], kc == 0, kc == 7,
                       [ska] + yak, [PS(0)])
                for kc in range(8):
                    mm(ps[1][:, 0:T], sb_[:, kc, cc * 128:(cc + 1) * 128], qT[:, kc, 0:T], kc == 0, kc == 7,
                       [skb] + ybk, [PS(1)])
                tt("dve", t1[:, 0:T], ps[0][:, 0:T], zsv[:, c, 0:T], ALU.mult, [PS(0)] + allz, ["t1"])
                tt("dve", t2[:, 0:T], ps[1][:, 0:T], gsv[:, c, 0:T], ALU.mult, [PS(1)] + allg, ["t2"])
                tt("dve", kT[:, c, 0:T], t1[:, 0:T], t2[:, 0:T], ALU.add, ["t1", "t2"], mk)
            ring_release()
            ring_release()
        ck("m9")
        P.tag = 'outp'
        for pi in range(2):
            slot, skey = ring_get("wo", pi)
            for (s, off, n) in subs:
                b = 4 + tmb % 2
                tmb += 1
                for kc in range(8):
                    mm(ps[b][:n, :], kT[:, kc, off:off + n], slot[:, kc, :], kc == 0, kc == 7, [skey] + mk, [PS(b)])
                tt("dve", h[:n, s, pi * 512:(pi + 1) * 512], ps[b][:n, :], h[:n, s, pi * 512:(pi + 1) * 512],
                   ALU.add, [PS(b), ("h", s)], [("h", s)])
            ring_release()


    def rms_groups(src, n, ngrp, gsz, skey):
        c0 = ncall["n"] % 64
        ncall["n"] += ngrp
        if c0 + ngrp > 64:
            c0 = 0
            ncall["n"] = ngrp
        for g in range(ngrp):
            act(junk[:n, 0:gsz], src[:n, g * gsz:(g + 1) * gsz], AF.Square, [skey], ["junk", ("ss", c0 + g)],
                accum_out=ss[:n, c0 + g:c0 + g + 1])
        rk = [("ss", c0 + g) for g in range(ngrp)]
        rr = [("rs", c0 + g) for g in range(ngrp)]
        act(rs[:n, c0:c0 + ngrp], ss[:n, c0:c0 + ngrp], AF.Sqrt, rk + ["eps_t"], rr, scale=1.0 / gsz, bias=eps_t[:n, 0:1])
        P.op("dve", lambda e: e.reciprocal(rs[:n, c0:c0 + ngrp], rs[:n, c0:c0 + ngrp]), rr, rr)
        return c0

    def ssd_chunk(s, off, n, meta, sbe):
        sl_ = slice(off, off + n)
        for half in range(2):
            b = half
            for cc in range(4):
                c = half * 4 + cc
                mm(ps[b][:n, cc * 128:(cc + 1) * 128], xsT[:, c, sl_], identb[:, :], True, True,
                   [("xsT", s), "identb"], [PS(b)])
            if not meta:
                cp("act", xstm[:n, half * 512:(half + 1) * 512], ps[b][:n, :], [PS(b)], ["xstm"])
            tt("dve", X[:n, half * 512:(half + 1) * 512].rearrange("p (h q) -> p h q", q=64),
               ps[b][:n, :].rearrange("p (h q) -> p h q", q=64),
               dtt[:n, s, half * 8:half * 8 + 8].unsqueeze(2).broadcast_to([n, 8, 64]), ALU.mult,
               [PS(b), ("dtt", s)], ["X"])
        yield
        mm(ps[4][:n, 0:16], trif[:n, :n], att_a[:n, s, :], True, True, ["trif", ("att_a", s)], [PS(4)])
        mm(ps[4][:, 16:32], onesf[:n, :], att_a[:n, s, :], True, True, ["onesf", ("att_a", s)], [PS(4)])
        act(ecs[:n, :], ps[4][:n, 0:16], AF.Exp, [PS(4)], ["ecs"])
        act(cdb[:, :], ps[4][:, 16:32], AF.Exp, [PS(4)], ["cdb"])
        for g in range(4):
            mm(ps[4][:n, g * 128:g * 128 + n], BT[:, g, sl_], CT[:, g, sl_], True, True,
               [("BT", s), ("CT", s)], [PS(4)])
        tt("dve", CBm[:n, :, 0:n], ps[4][:n, :].rearrange("p (g l) -> p g l", g=4)[:, :, 0:n],
           trif[:n, 0:n].unsqueeze(1).broadcast_to([n, 4, n]), ALU.mult, [PS(4), "trif"], ["CBm"])
        yield
        for half in range(2):
            tt(sbe, R[:n, :, 0:n], att_a[:n, s, half * 8:half * 8 + 8].unsqueeze(2).broadcast_to([n, 8, n]),
               trif[:n, 0:n].unsqueeze(1).broadcast_to([n, 8, n]), ALU.mult, [("att_a", s), "trif"], ["R"])
            for q4 in range(2):
                b = 2 + q4
                for hh in range(4):
                    mm(ps[b][:n, hh * 128:hh * 128 + n], slf[:n, :n], R[:n, q4 * 4 + hh, 0:n], True, True,
                       ["slf", "R"], [PS(b)])
                act(L[:n, q4 * 4:q4 * 4 + 4, 0:n], ps[b][:n, :].rearrange("p (h l) -> p h l", h=4)[:, :, 0:n],
                    AF.Exp, [PS(b)], ["L"])
            if not meta:
                tt(sbe, Mt[:n, half * 8:half * 8 + 8, 0:n].rearrange("p (g r) l -> p g r l", g=2),
                   L[:n, :, 0:n].rearrange("p (g r) l -> p g r l", g=2),
                   CBm[:n, half * 2:half * 2 + 2, 0:n].unsqueeze(2).broadcast_to([n, 2, 4, n]), ALU.mult,
                   ["L", "CBm"], [("Mt", half)])
            tt(sbe, Xd[:n, half * 512:(half + 1) * 512].rearrange("p (h q) -> p h q", q=64),
               X[:n, half * 512:(half + 1) * 512].rearrange("p (h q) -> p h q", q=64),
               L[:n, :, n - 1:n].broadcast_to([n, 8, 64]), ALU.mult, ["L", "X"], ["Xd"])
            yield
        for g in range(4):
            mm(ps[4][:n, g * 128:(g + 1) * 128], BT[:, g, sl_], identb[:, :], True, True, [("BT", s), "identb"], [PS(4)])
        cp("act", Btm[:n, :], ps[4][:n, :], [PS(4)], ["Btm"])
        yield
        if not meta:
            for hh in range(16):
                b = hh // 8
                mm(ps[b][:n, (hh % 8) * 64:(hh % 8) * 64 + 64], Mt[:n, hh, 0:n], X[:n, hh * 64:(hh + 1) * 64],
                   True, True, [("Mt", hh // 8), "X"], [PS(b)])
            for g in range(4):
                b = 2 + g // 2
                mm(ps[b][:n, (g % 2) * 256:(g % 2) * 256 + 256], CT[:, g, sl_], ssdSbf[:, g * 256:(g + 1) * 256],
                   True, True, [("CT", s), "ssdSbf"], [PS(b)])
            yield
            for half in range(2):
                hs = slice(half * 512, (half + 1) * 512)
                tt("dve", y1[:n, hs].rearrange("p (h q) -> p h q", q=64),
                   ps[2 + half][:n, :].rearrange("p (h q) -> p h q", q=64),
                   ecs[:n, half * 8:half * 8 + 8].unsqueeze(2).broadcast_to([n, 8, 64]), ALU.mult,
                   [PS(2 + half), "ecs"], ["y1"])
            tt("dve", y2[:n, :].rearrange("p (h q) -> p h q", q=64),
               xstm[:n, :].rearrange("p (h q) -> p h q", q=64),
               D_bc[:n, :].unsqueeze(2).broadcast_to([n, 16, 64]), ALU.mult, ["xstm", "D_bc"], ["y2"])
            yield
        for g in range(4):
            b = 2 + g // 2
            mm(ps[b][:, (g % 2) * 256:(g % 2) * 256 + 256], Btm[:n, g * 128:(g + 1) * 128], Xd[:n, g * 256:(g + 1) * 256],
               True, True, ["Btm", "Xd"], [PS(b)])
        tt(sbe, ssdS[:, :].rearrange("p (h q) -> p h q", q=64), ssdS[:, :].rearrange("p (h q) -> p h q", q=64),
           cdb[:, :].unsqueeze(2).broadcast_to([128, 16, 64]), ALU.mult, ["ssdS", "cdb"], ["ssdS"])
        yield
        for half in range(2):
            hs = slice(half * 512, (half + 1) * 512)
            tt("dve", ssdS[:, hs], ps[2 + half][:, :], ssdS[:, hs], ALU.add, [PS(2 + half), "ssdS"], ["ssdS"])
        cp("act", ssdSbf[:, :], ssdS[:, :], ["ssdS"], ["ssdSbf"])
        yield
        if not meta:
            tt("dve", y1[:n, :], y1[:n, :], y2[:n, :], ALU.add, ["y1", "y2"], ["y1"])
            for half in range(2):
                hs = slice(half * 512, (half + 1) * 512)
                tt("dve", y1[:n, hs], ps[half][:n, :], y1[:n, hs], ALU.add, [PS(half), "y1"], ["y1"])
            yield
            tt("dve", y1[:n, :], y1[:n, :], zs[:n, s, :], ALU.mult, ["y1", ("zs", s)], ["y1"])
            c0 = rms_groups(y1, n, 4, 256, "y1")
            yield
            tt("dve", ytm[:n, :].rearrange("p (g q) -> p g q", q=256), y1[:n, :].rearrange("p (g q) -> p g q", q=256),
               rs[:n, c0:c0 + 4].unsqueeze(2).broadcast_to([n, 4, 256]), ALU.mult,
               ["y1"] + [("rs", c0 + g) for g in range(4)], ["xn_tm"])
            transpose_tm(ytm, n, "xn_tm", xsT, "xsT", s, off, 3, banks=(0, 1))
            yield

    def hgrn_chunk(s, off, n, meta, sbe):
        sl_ = slice(off, off + n)
        ytm2 = attm[:, :, :].rearrange("p h l -> p (h l)")
        if not meta:
            for hd in range(8):
                b = 5 + hd // 4
                mm(ps[b][:n, (hd % 4) * 128:(hd % 4) * 128 + n], kT[:, hd, sl_], qT[:, hd, sl_], True, True,
                   [("kT", s), ("qT", s)], [PS(b)])
            for half in range(2):
                tt("dve", attm[:n, half * 4:half * 4 + 4, 0:n],
                   ps[5 + half][:n, :].rearrange("p (g l) -> p g l", g=4)[:, :, 0:n],
                   trif[:n, 0:n].unsqueeze(1).broadcast_to([n, 4, n]), ALU.mult, [PS(5 + half), "trif"], ["attm"])
            yield
        for half in range(2):
            for h4 in range(4):
                hd = half * 4 + h4
                mm(ps[7][:n, h4 * 128:h4 * 128 + 128], kT[:, hd, sl_], identb[:, :], True, True,
                   [("kT", s), "identb"], [PS(7)])
            cp("act", ktm[:n, half * 512:(half + 1) * 512], ps[7][:n, :], [PS(7)], ["ktm"])
        yield
        if not meta:
            tt(sbe, Sbf[:, :].rearrange("p (h v) -> p h v", v=128), hgS[:, :].rearrange("p (h v) -> p h v", v=128),
               gsc[:, :, s, 0:1].broadcast_to([128, 8, 128]), ALU.mult, ["hgS", "gsc"], ["Sbf"])
            yield
            for hd in range(8):
                b = 5 + hd // 4
                o_ = ps[b][:n, (hd % 4) * 128:(hd % 4) * 128 + 128]
                mm(o_, qT[:, hd, sl_], Sbf[:, hd * 128:(hd + 1) * 128], True, False, [("qT", s), "Sbf"], [PS(b)])
                mm(o_, attm[:n, hd, 0:n], vtm[:n, s, hd * 128:(hd + 1) * 128], False, True,
                   ["attm", ("vtm", s)], [PS(b)])
            yield
            for half in range(2):
                cp("act", y3[:n, half * 512:(half + 1) * 512], ps[5 + half][:n, :], [PS(5 + half)], ["y3"])
            c0 = rms_groups(y3, n, 8, 128, "y3")
            yield
        tt(sbe, hgS[:, :].rearrange("p (h v) -> p h v", v=128), hgS[:, :].rearrange("p (h v) -> p h v", v=128),
           gsc[:, :, s, 1:2].broadcast_to([128, 8, 128]), ALU.mult, ["hgS", "gsc"], ["hgS"])
        yield
        for half in range(2):
            for h4 in range(4):
                hd = half * 4 + h4
                mm(ps[7][:, h4 * 128:h4 * 128 + 128], ktm[:n, hd * 128:(hd + 1) * 128],
                   vtm[:n, s, hd * 128:(hd + 1) * 128], True, True, ["ktm", ("vtm", s)], [PS(7)])
            for h4 in range(4):
                hd = half * 4 + h4
                stt(hgS[:, hd * 128:(hd + 1) * 128], ps[7][:, h4 * 128:h4 * 128 + 128], gsc[:, hd, s, 2:3],
                    hgS[:, hd * 128:(hd + 1) * 128], ALU.mult, ALU.add, [PS(7), "hgS", "gsc"], ["hgS"])
            yield
        if not meta:
            tt("dve", y3[:n, :].rearrange("p (g q) -> p g q", q=128), y3[:n, :].rearrange("p (g q) -> p g q", q=128),
               rs[:n, c0:c0 + 8].unsqueeze(2).broadcast_to([n, 8, 128]), ALU.mult,
               ["y3"] + [("rs", c0 + g) for g in range(8)], ["y3"])
            yield
            tt(sbe, ytm2[:n, :], y3[:n, :], gs[:n, s, :], ALU.mult, ["y3", ("gs", s)], ["attm"])
            yield
            transpose_tm(ytm2, n, "attm", qT, "qT", s, off, 4, banks=(5, 6))
            yield

    msubs = [(0, 0, NMETA)]
    dma("sp", h[:NMETA, 0, :], meta_d[:, :], (), [("h", 0)])
    if stop == 2:
        P.emit(); return nc
    ffn(msubs, NMETA, "gu1", "dn1", 0)
    if stop == 3:
        P.emit(); return nc
    mixer(msubs, NMETA, True)
    if stop == 4:
        P.emit(); return nc
    dma("sp", m_ssdS[:, :], ssdS[:, :], ["ssdS"], ["m_ssdS"])
    dma("sp", m_hgS[:, :], hgS[:, :], ["hgS"], ["m_hgS"])
    cp("dve", m_ctail[:, :, :], ctail[:, :, :], [("ctail", ch) for ch in range(16)], ["m_ctail"])
    subs = [(s, s * 128, 128) for s in range(4)]
    rstate["main"] = True
    for q in range(nseq):
        dma("sp", ssdS[:, :], m_ssdS[:, :], ["m_ssdS"], ["ssdS"])
        cp("dve", ssdSbf[:, :], ssdS[:, :], ["ssdS"], ["ssdSbf"])
        dma("sp", hgS[:, :], m_hgS[:, :], ["m_hgS"], ["hgS"])
        cp("dve", ctail[:, :, :], m_ctail[:, :, :], ["m_ctail"], [("ctail", ch) for ch in range(16)])
        for ti in range(ntile):
            t0 = ti * TT
            first = (q == 0 and ti == 0)
            if first:
                for s4 in range(4):
                    dma("sp", h[:, s4, :], x_d[q, t0 + s4 * 128:t0 + (s4 + 1) * 128, :], (), [("h", s4)])
                norm_T(subs, 0, xnT, "xnT")
            ffn(subs, TT, "gu1", "dn1", 0, prenormed=True)
            if stop == 5:
                P.emit(); return nc
            try:
                mixer(subs, TT, False)
            except StopBuild:
                P.emit(); return nc
            if stop == 6:
                P.emit(); return nc
            ffn(subs, TT, "gu2", "dn2", 2)
            if stop == 7:
                P.emit(); return nc
            if ti + 1 < ntile:
                nxt = (q, (ti + 1) * TT)
            elif q + 1 < nseq:
                nxt = (q + 1, 0)
            else:
                nxt = None
            for (s, off, n) in subs:
                P.tag = 'final'
                c = ncall["n"] % 64
                ncall["n"] += 1
                yo = y1 if s % 2 == 0 else y2
                yk = "y1" if s % 2 == 0 else "y2"
                act(junk[:n, :], h[:n, s, :], AF.Square, [("h", s)], ["junk", ("ss", c)], accum_out=ss[:n, c:c + 1])
                act(rs[:n, c:c + 1], ss[:n, c:c + 1], AF.Sqrt, [("ss", c), "eps_t"], [("rs", c)], scale=1.0 / D,
                    bias=eps_t[:n, 0:1])
                P.op("dve", lambda e, c=c: e.reciprocal(rs[:, c:c + 1], rs[:, c:c + 1]), [("rs", c)], [("rs", c)])
                stt(yo[:, :], h[:, s, :], rs[:, c:c + 1], fnorm_bc[:, :], ALU.mult, ALU.mult,
                    [("h", s), ("rs", c), "fnorm_bc"], [yk])
                dma("sp", out_d[q, t0 + off:t0 + off + 128, :], yo[:, :], [yk], ())
                if nxt is not None:
                    nq, nt0 = nxt
                    dma("sp", h[:, s, :], x_d[nq, nt0 + off:nt0 + off + 128, :], (), [("h", s)])
            if nxt is not None:
                norm_T(subs, 0, xnT, "xnT")
    assert rstate["next_get"] == len(plan), (rstate["next_get"], len(plan))
    P.emit()
    return nc


_NC_CACHE = {}


def kernel(**inputs):
    ncores = 8
    x = np.ascontiguousarray(inputs["x"], dtype=np.float32)
    B = x.shape[0]
    per = B // ncores
    if "nc" not in _NC_CACHE:
        _NC_CACHE["nc"] = build_nc(nseq=per, ntile=4)
    nc = _NC_CACHE["nc"]
    shared = {}
    for k, v in inputs.items():
        if k == "x":
            continue
        a = np.ascontiguousarray(np.asarray(v, dtype=np.float32))
        if k == "final_norm" or k == "meta_tokens" or k == "hg_lower_bound":
            shared[k] = a
        else:
            shared[k] = np.ascontiguousarray(a[0])
    in_maps = []
    for c in range(ncores):
        m = dict(shared)
        m["x"] = np.ascontiguousarray(x[c * per:(c + 1) * per])
        in_maps.append(m)
    res = run_bass_kernel_spmd(nc, in_maps, core_ids=list(range(ncores)))
    return np.concatenate([np.asarray(r["out"]) for r in res.results], axis=0).astype(np.float32)
```

```python
import numpy as np
import concourse.bass as bass
import concourse.mybir as mybir
from concourse.bass_utils import run_bass_kernel_spmd

F32 = mybir.dt.float32
BF16 = mybir.dt.bfloat16
AF = mybir.ActivationFunctionType
ALU = mybir.AluOpType

D = 1024
SEQ = 2048
NMETA = 16
DFF = 2816
NJ = 22
EPS = 1e-6
W_IN_COLS = 9232
SAME_ENGINE_SYNC = True


class Prog:
    ENGS = ("pe", "act", "dve", "pool", "sp")
    EPOCH = 3000

    def __init__(self, nc, nd):
        self.nc = nc
        self.ops = []
        self.eops = {e: [] for e in self.ENGS}
        self.last_w = {}
        self.rd_eng = {}
        self.rd_dma = {}
        self.seen = {e: {f: -1 for f in self.ENGS} for e in self.ENGS}
        self.dma_seen = {e: set() for e in self.ENGS}
        self.dma_ops = {e: [] for e in self.ENGS}
        self.ND = nd

    def op(self, eng, fn, reads=(), writes=(), dma=False):
        idx = len(self.ops)
        o = {"eng": eng, "fn": fn, "dma": dma, "idx": idx, "eidx": len(self.eops[eng]),
             "sig": False, "waits": [], "tag": getattr(self, "tag", "")}
        raw = set()
        oth = set()
        psr = [k for k in reads if isinstance(k, tuple) and k[0] == "ps"]
        if psr:
            writes = list(writes) + [k for k in psr if k not in writes]
        for k in reads:
            w = self.last_w.get(k)
            if w is not None:
                raw.add(w)
        for k in writes:
            w = self.last_w.get(k)
            if w is not None:
                oth.add(w)
            r = self.rd_eng.get(k)
            if r:
                oth.update(r.values())
            r = self.rd_dma.get(k)
            if r:
                oth.update(r)
        if dma:
            i = len(self.dma_ops[eng])
            o["dma_i"] = i
            if i >= self.ND[eng]:
                oth.add(self.dma_ops[eng][i - self.ND[eng]])
            self.dma_ops[eng].append(idx)
        best = {}
        for d in raw | oth:
            y = self.ops[d]
            if y["dma"]:
                if d in self.dma_seen[eng]:
                    continue
                self.dma_seen[eng].add(d)
                y["sig"] = True
                o["waits"].append(d)
            else:
                f = y["eng"]
                if f == eng and not dma:
                    if eng == "pe" or not SAME_ENGINE_SYNC:
                        continue
                if self.seen[eng][f] >= y["eidx"]:
                    continue
                if f not in best or self.ops[best[f]]["eidx"] < y["eidx"]:
                    best[f] = d
        for f, d in best.items():
            y = self.ops[d]
            self.seen[eng][f] = y["eidx"]
            y["sig"] = True
            o["waits"].append(d)
        for k in reads:
            if dma:
                self.rd_dma.setdefault(k, []).append(idx)
            else:
                self.rd_eng.setdefault(k, {})[eng] = idx
        for k in writes:
            self.last_w[k] = idx
            self.rd_eng[k] = {}
            self.rd_dma[k] = []
        self.ops.append(o)
        self.eops[eng].append(idx)
        return idx

    def emit(self):
        nc = self.nc
        dsem = {}
        for e in self.ENGS:
            if self.dma_ops[e]:
                dsem[e] = [nc.alloc_semaphore("d%s%d" % (e, i)) for i in range(self.ND[e])]
        csem = {}
        for e in self.ENGS:
            cnt = 0
            for idx in self.eops[e]:
                o = self.ops[idx]
                if o["dma"]:
                    i = o["dma_i"]
                    o["sem"] = dsem[e][i % self.ND[e]]
                    o["val"] = 16 * (i // self.ND[e] + 1)
                elif o["sig"]:
                    ep = cnt // self.EPOCH
                    if (e, ep) not in csem:
                        csem[(e, ep)] = nc.alloc_semaphore("c%s%d" % (e, ep))
                    o["sem"] = csem[(e, ep)]
                    o["val"] = cnt % self.EPOCH + 1
                    cnt += 1
        ops = self.ops

        def run(e, engine):
            for idx in self.eops[e]:
                o = ops[idx]
                for d in o["waits"]:
                    y = ops[d]
                    engine.wait_ge(y["sem"], y["val"])
                ins = o["fn"](engine)
                if o["dma"]:
                    ins.then_inc(o["sem"], 16)
                elif o["sig"]:
                    ins.then_inc(o["sem"], 1)
            if e == "sp":
                for ee in self.ENGS:
                    n = len(self.dma_ops[ee])
                    if n == 0:
                        continue
                    nd = self.ND[ee]
                    for j in range(min(nd, n)):
                        cntj = (n - 1 - j) // nd + 1
                        engine.wait_ge(dsem[ee][j], 16 * cntj)

        with nc.Block() as block:
            @block.sync
            def _(eng):
                run("sp", eng)

            @block.tensor
            def _(eng):
                run("pe", eng)

            @block.scalar
            def _(eng):
                run("act", eng)

            @block.vector
            def _(eng):
                run("dve", eng)

            @block.gpsimd
            def _(eng):
                run("pool", eng)


def build_nc(nseq=2, ntile=4, debug=None, stop=None):
    nc = bass.Bass("TRN2", target_bir_lowering=False)
    P = Prog(nc, {"sp": 40, "pool": 12, "act": 4, "dve": 4, "pe": 4})
    TT = 512

    def din(name, shape):
        return nc.dram_tensor(name, list(shape), F32, kind="ExternalInput").ap()

    x_d = din("x", [nseq, SEQ, D])
    meta_d = din("meta_tokens", [NMETA, D])
    ffn1_norm_d = din("ffn1_norm", [D])
    ffn1_gu_d = din("ffn1_w_gu", [D, 2 * DFF])
    ffn1_dn_d = din("ffn1_w_down", [DFF, D])
    mix_norm_d = din("mix_norm", [D])
    w_in_d = din("w_in", [D, W_IN_COLS])
    conv_w_d = din("ssd_conv_w", [4, 2048])
    conv_b_d = din("ssd_conv_b", [2048])
    dt_bias_d = din("ssd_dt_bias", [16])
    a_log_d = din("ssd_a_log", [16])
    dskip_d = din("ssd_d", [16])
    ssd_norm_d = din("ssd_norm", [D])
    hg_lb_d = din("hg_lower_bound", [2, D])
    hg_norm_d = din("hg_norm", [D])
    wa_d = din("w_branch_a", [D, D])
    wb_d = din("w_branch_b", [D, D])
    wo_d = din("w_out", [D, D])
    ffn2_norm_d = din("ffn2_norm", [D])
    ffn2_gu_d = din("ffn2_w_gu", [D, 2 * DFF])
    ffn2_dn_d = din("ffn2_w_down", [DFF, D])
    fnorm_d = din("final_norm", [D])
    out_d = nc.dram_tensor("out", [nseq, SEQ, D], F32, kind="ExternalOutput").ap()

    def dscr(name, shape):
        return nc.dram_tensor(name, list(shape), BF16, kind="Internal").ap()

    scr = {
        "gu1": dscr("s_gu1", [11, 128, 4096]), "gu2": dscr("s_gu2", [11, 128, 4096]),
        "win": dscr("s_win", [18, 128, 4096]),
        "wa": dscr("s_wa", [2, 128, 4096]), "wb": dscr("s_wb", [2, 128, 4096]),
        "wo": dscr("s_wo", [2, 128, 4096]),
        "dn1": dscr("s_dn1", [NJ, 128, 1024]), "dn2": dscr("s_dn2", [NJ, 128, 1024]),
    }

    def sb(name, shape, dt=F32):
        return nc.alloc_sbuf_tensor(name, list(shape), dt)

    identf = sb("identf", [128, 128]); identb = sb("identb", [128, 128], BF16)
    trif = sb("trif", [128, 128]); slf = sb("slf", [128, 128]); onesf = sb("onesf", [128, 128])
    fnorm_bc = sb("fnorm_bc", [128, D])
    PR = sb("PR", [8, 2048]); NR = sb("NR", [8, D])
    PT = sb("PT", [128, 16, 8]); NT = sb("NT", [128, 8, 8])
    lbp = sb("lbp", [128, 8, 3])
    dtb_bc = sb("dtb_bc", [128, 16]); A_bc = sb("A_bc", [128, 16]); D_bc = sb("D_bc", [128, 16])
    wdt = sb("wdt", [128, 8, 16], BF16)
    h = sb("h", [128, 4, D])
    xnT = sb("xnT", [128, 8, TT], BF16)
    xn_tm = sb("xn_tm", [128, D], BF16)
    junk = sb("junk", [128, D], BF16)
    ss = sb("ss", [128, 64]); rs = sb("rs", [128, 64])
    actT = sb("actT", [128, 6, TT], BF16)
    wd = sb("wd", [128, 6, D], BF16)
    NSLOT = 3
    ring = [sb("ring%d" % i, [128, 8, 512], BF16) for i in range(NSLOT)]
    xsT = sb("xsT", [128, 8, TT], BF16)
    BT = sb("BT", [128, 4, TT], BF16); CT = sb("CT", [128, 4, TT], BF16)
    zs = sb("zs", [128, 4, D], BF16); gs = sb("gs", [128, 4, D], BF16)
    qT = sb("qT", [128, 8, TT], BF16); kT = sb("kT", [128, 8, TT], BF16)
    vtm = sb("vtm", [128, 4, D], BF16)
    xpad = sb("xpad", [128, TT + 3]); cacc = sb("cacc", [128, TT])
    ctail = sb("ctail", [128, 16, 3]); m_ctail = sb("m_ctail", [128, 16, 3])
    t1 = sb("t1", [128, TT]); t2 = sb("t2", [128, TT]); t3 = sb("t3", [128, TT])
    gsc = sb("gsc", [128, 8, 4, 4])
    dtt = sb("dtt", [128, 4, 16]); att_a = sb("att_a", [128, 4, 16]); dtmp = sb("dtmp", [128, 16])
    ecs = sb("ecs", [128, 16]); cdb = sb("cdb", [128, 16])
    R = sb("R", [128, 8, 128]); L = sb("L", [128, 8, 128])
    Mt = sb("Mt", [128, 16, 128], BF16)
    CBm = sb("CBm", [128, 4, 128])
    X = sb("X", [128, D], BF16); Xd = sb("Xd", [128, D], BF16); xstm = sb("xstm", [128, D], BF16)
    Btm = sb("Btm", [128, 512], BF16)
    y1 = sb("y1", [128, D]); y2 = sb("y2", [128, D]); y3 = sb("y3", [128, D])
    gr = sb("gr", [128, 8, 4, 2])
    attm = sb("attm", [128, 8, 128], BF16); ktm = sb("ktm", [128, D], BF16)
    Sbf = sb("Sbf", [128, D], BF16)
    ssdS = sb("ssdS", [128, D]); ssdSbf = sb("ssdSbf", [128, D], BF16); hgS = sb("hgS", [128, D])
    m_ssdS = nc.dram_tensor("m_ssdS", [128, D], F32, kind="Internal").ap()
    m_hgS = nc.dram_tensor("m_hgS", [128, D], F32, kind="Internal").ap()
    ytm = xn_tm
    sg = t1
    wdf = wd[:, :, :].rearrange("p j f -> p (j f)").bitcast(F32)
    xpad2 = wdf[:, 0:TT + 3]; cacc2 = wdf[:, 516:516 + TT]
    t1b = wdf[:, 1028:1028 + TT]; t2b = wdf[:, 1540:1540 + TT]; t3b = wdf[:, 2052:2052 + TT]
    dummy = sb("fdummy", [128, 2])
    xpadb = xpad[:, :].bitcast(BF16)
    xpbs = [xpadb[:, 0:TT + 3], xpadb[:, TT + 3:2 * (TT + 3)]]
    caccb = cacc[:, :].bitcast(BF16)
    dgs = [[caccb[:, (k * 4 + j) * 128:(k * 4 + j + 1) * 128] for j in range(4)] for k in range(2)]
    ALIAS_KEYS = ["xpad2", "cacc2", "t1b", "t2b", "t3b"]

    def fence(reads, writes):
        P.op("dve", lambda e: e.memset(dummy[:, 0:1], 0.0), list(reads), list(writes) + ["dummy"])
    ps = [nc.alloc_psum_tensor("ps%d" % i, [128, 512], F32) for i in range(8)]

    def PS(b):
        return ("ps", b)

    class StopBuild(Exception):
        pass

    def ck(name):
        if stop == name and rstate.get("main"):
            raise StopBuild()

    def act(out, in_, func, reads, writes, **kw):
        P.op("act", lambda e: e.activation(out, in_, func, **kw), reads, writes)

    def tt(eng, out, a, b, op, reads, writes):
        P.op(eng, lambda e: e.tensor_tensor(out, a, b, op), reads, writes)

    def ts(eng, out, a, s1, s2, op0, op1, reads, writes):
        if s2 is None:
            P.op(eng, lambda e: e.tensor_scalar(out, a, s1, None, op0), reads, writes)
        else:
            P.op(eng, lambda e: e.tensor_scalar(out, a, s1, s2, op0, op1), reads, writes)

    def stt(out, a, s, b, op0, op1, reads, writes):
        P.op("dve", lambda e: e.scalar_tensor_tensor(out, a, s, b, op0, op1), reads, writes)

    def cp(eng, out, in_, reads, writes):
        if eng == "act":
            P.op("act", lambda e: e.copy(out, in_), reads, writes)
        else:
            P.op(eng, lambda e: e.tensor_copy(out, in_), reads, writes)

    def mm(out, lhsT, rhs, start, stop, reads, writes):
        i = P.op("pe", lambda e: e.matmul(out, lhsT, rhs, start=start, stop=stop), reads, writes)
        P.ops[i]["f32"] = (lhsT.dtype == F32)

    def dma(eng, out, in_, reads, writes):
        P.op(eng, lambda e: e.dma_start(out=out, in_=in_), reads, writes, dma=True)

    eps_t = sb("eps_t", [128, 1])
    one_t = sb("one_t", [128, 1])
    P.op("pool", lambda e: e.memset(eps_t[:, :], EPS), (), ["eps_t"])
    P.op("pool", lambda e: e.memset(one_t[:, :], 1.0), (), ["one_t"])
    P.op("pool", lambda e: e.memset(onesf[:, :], 1.0), (), ["onesf"])
    for tname, t, pat, cm, cmpop in (("identf", identf, [[-1, 128]], 1, ALU.is_equal),
                                     ("trif", trif, [[1, 128]], -1, ALU.is_ge),
                                     ("slf", slf, [[-1, 128]], 1, ALU.is_gt)):
        P.op("pool", lambda e, t=t, pat=pat, cm=cm, cmpop=cmpop: e.affine_select(
            t[:, :], onesf[:, :], pat, cmpop, 0.0, base=0, channel_multiplier=cm),
            ["onesf"], [tname])
    cp("pool", identb[:, :], identf[:, :], ["identf"], ["identb"])
    P.op("pool", lambda e: e.memset(PR[:, :], 0.0), (), ["PR"])
    P.op("pool", lambda e: e.memset(NR[:, :], 0.0), (), ["NR"])
    P.op("pool", lambda e: e.memset(ssdS[:, :], 0.0), (), ["ssdS"])
    P.op("pool", lambda e: e.memset(ssdSbf[:, :], 0.0), (), ["ssdSbf"])
    P.op("pool", lambda e: e.memset(hgS[:, :], 0.0), (), ["hgS"])
    P.op("pool", lambda e: e.memset(ctail[:, :, :], 0.0), (), ["ctail"])
    dma("pool", PR[0:4, :], conv_w_d[:, :], ["PR"], ["PR"])
    dma("pool", PR[4:5, :], conv_b_d[None, :], ["PR"], ["PR"])
    dma("pool", PR[5:7, 0:D], hg_lb_d[:, :], ["PR"], ["PR"])
    for i, nd_ in enumerate((ffn1_norm_d, mix_norm_d, ffn2_norm_d, ssd_norm_d, hg_norm_d)):
        dma("pool", NR[i:i + 1, :], nd_[None, :], ["NR"], ["NR"])
    dma("pool", fnorm_bc[:, :], fnorm_d.partition_broadcast(128), (), ["fnorm_bc"])
    dma("pool", dtb_bc[:, :], dt_bias_d.partition_broadcast(128), (), ["dtb_bc"])
    dma("pool", A_bc[:, :], a_log_d.partition_broadcast(128), (), ["A_bc"])
    dma("pool", D_bc[:, :], dskip_d.partition_broadcast(128), (), ["D_bc"])
    dma("pool", wdt[:, :, :], w_in_d[:, 3072:3088].rearrange("(kc p) c -> p kc c", p=128), (), ["wdt"])
    def cast_gu(name, src):
        for i in range(11):
            dst = scr[name][i].rearrange("p (kc c) -> p kc c", kc=8)
            dma("pool", dst[:, :, 0:256],
                src[:, 256 * i:256 * i + 256].rearrange("(kc p) c -> p kc c", p=128), (), [("scrh", name, i, 0)])
            dma("pool", dst[:, :, 256:512],
                src[:, DFF + 256 * i:DFF + 256 * i + 256].rearrange("(kc p) c -> p kc c", p=128), (),
                [("scrh", name, i, 1)])

    def cast_dn(name, src):
        for g0 in range(0, NJ, 6):
            g1 = min(NJ, g0 + 6)
            dma("pool", scr[name][g0:g1].rearrange("j p f -> p j f"),
                src[g0 * 128:g1 * 128, :].rearrange("(j p) f -> p j f", p=128), (), [("scr", name, g0)])

    def cast_sq(name, src, cols, order=None):
        for i in (order if order is not None else range(len(cols))):
            c0 = cols[i]
            dma("pool", scr[name][i].rearrange("p (kc c) -> p kc c", kc=8),
                src[:, c0:c0 + 512].rearrange("(kc p) c -> p kc c", p=128), (), [("scr", name, i)])

    win_cols = [1024, 1536, 2048, 2560, 3088, 3600, 4112, 4624, 5136, 5648, 0, 512, 6160, 6672,
                7184, 7696, 8208, 8720]
    cast_gu("gu1", ffn1_gu_d)
    cast_dn("dn1", ffn1_dn_d)
    cast_sq("win", w_in_d, win_cols, order=[6, 7, 0, 1, 2, 3, 8, 9, 4, 5, 10, 11, 12, 13, 14, 15, 16, 17])
    cast_sq("wa", wa_d, [0, 512]); cast_sq("wb", wb_d, [0, 512]); cast_sq("wo", wo_d, [0, 512])
    cast_gu("gu2", ffn2_gu_d)
    cast_dn("dn2", ffn2_dn_d)

    if stop == 0:
        P.emit(); return nc
    for ch in range(16):
        mm(ps[0][:, ch * 8:ch * 8 + 8], PR[0:8, ch * 128:(ch + 1) * 128], identf[0:8, 0:8], True, True,
           ["PR", "identf"], [PS(0)])
    cp("dve", PT[:, :, :], ps[0][:, 0:128].rearrange("p (c r) -> p c r", r=8), [PS(0)], ["PT"])
    for ch in range(8):
        mm(ps[1][:, ch * 8:ch * 8 + 8], NR[0:8, ch * 128:(ch + 1) * 128], identf[0:8, 0:8], True, True,
           ["NR", "identf"], [PS(1)])
    cp("dve", NT[:, :, :], ps[1][:, 0:64].rearrange("p (c r) -> p c r", r=8), [PS(1)], ["NT"])
    tt("dve", lbp[:, :, 2], PT[:, 0:8, 5], PT[:, 0:8, 6], ALU.subtract, ["PT"], ["lbp"])
    act(lbp[:, :, 0], lbp[:, :, 2], AF.Sigmoid, ["lbp"], ["lbp"])
    ts("dve", lbp[:, :, 1], lbp[:, :, 0], -1.0, 1.0, ALU.mult, ALU.add, ["lbp"], ["lbp"])
    ts("dve", lbp[:, :, 2], lbp[:, :, 1], -1.0, None, ALU.mult, None, ["lbp"], ["lbp"])
    act(A_bc[:, :], A_bc[:, :], AF.Exp, ["A_bc"], ["A_bc"])
    ts("dve", A_bc[:, :], A_bc[:, :], -1.0, None, ALU.mult, None, ["A_bc"], ["A_bc"])

    if stop == 1:
        P.emit(); return nc
    def mixer_sched(meta):
        if meta:
            return [("f", 0), ("f", 1), ("conv", 0), ("conv", 1), ("conv", 2), ("conv", 3), ("i", 0), ("i", 1)]
        return [("f", 0), ("f", 1), ("conv", 0), ("conv", 1), ("conv", 2), ("conv", 3), ("q", 0), ("q", 1),
                ("i", 0), ("i", 1), ("z", 0), ("z", 1), ("g", 0), ("g", 1)]

    def tile_panels(meta):
        seq = [("gu1", i) for i in range(11)]
        wbase = {"conv": 0, "q": 4, "f": 6, "i": 8, "z": 10, "g": 12}
        seq += [("win", wbase[k] + a) for (k, a) in mixer_sched(meta)]
        if not meta:
            seq += [("win", 14), ("win", 15), ("win", 16), ("win", 17)]
            seq += [("wa", 0), ("wb", 0), ("wa", 1), ("wb", 1), ("wo", 0), ("wo", 1)]
            seq += [("gu2", i) for i in range(11)]
        return seq

    plan = tile_panels(True)
    for _ in range(nseq * ntile):
        plan += tile_panels(False)
    rstate = {"next_load": 0, "next_get": 0}

    def ring_load():
        i = rstate["next_load"]
        if i >= len(plan):
            return
        name, pi = plan[i]
        slot = ring[i % NSLOT]
        rk = [("scrh", name, pi, 0), ("scrh", name, pi, 1)] if name.startswith("gu") else [("scr", name, pi)]
        dma("sp", slot[:, :, :], scr[name][pi].rearrange("p (kc c) -> p kc c", kc=8), rk, [("ring", i % NSLOT)])
        rstate["next_load"] = i + 1

    for _ in range(NSLOT):
        ring_load()

    def ring_get(name, pi):
        i = rstate["next_get"]
        assert plan[i] == (name, pi), (plan[i], name, pi)
        rstate["next_get"] = i + 1
        return ring[i % NSLOT], ("ring", i % NSLOT)

    def ring_release():
        ring_load()

    ncall = {"n": 0}

    def norm_T(subs, wrow, dstT, dkey):
        P.tag = 'norm'
        for (s, off, n) in subs:
            c = ncall["n"] % 64
            ncall["n"] += 1
            act(junk[:n, :], h[:n, s, :], AF.Square, [("h", s)], ["junk", ("ss", c)], accum_out=ss[:n, c:c + 1])
            act(rs[:n, c:c + 1], ss[:n, c:c + 1], AF.Sqrt, [("ss", c), "eps_t"], [("rs", c)], scale=1.0 / D, bias=eps_t[:n, 0:1])
            P.op("dve", lambda e, c=c, n=n: e.reciprocal(rs[:n, c:c + 1], rs[:n, c:c + 1]), [("rs", c)], [("rs", c)])
            ts("dve", xn_tm[:n, :], h[:n, s, :], rs[:n, c:c + 1], None, ALU.mult, None,
               [("h", s), ("rs", c)], ["xn_tm"])
            transpose_tm(xn_tm, n, "xn_tm", dstT, dkey, s, off, wrow)

    def transpose_tm(src, n, skey, dstT, dkey, s, off, wrow, banks=(6, 7)):
        for half in range(2):
            b = banks[half]
            for cc in range(4):
                c = half * 4 + cc
                mm(ps[b][:, cc * 128:cc * 128 + n], src[:n, c * 128:(c + 1) * 128], identb[:n, :n], True, True,
                   [skey, "identb"], [PS(b)])
            tt("dve", dstT[:, half * 4:half * 4 + 4, off:off + n],
               ps[b][:, :].rearrange("p (c t) -> p c t", c=4)[:, :, 0:n],
               NT[:, half * 4:half * 4 + 4, wrow:wrow + 1].broadcast_to([128, 4, n]), ALU.mult,
               [PS(b), "NT"], [(dkey, s)])


    dcnt = {"n": 0}

    def ffn(subs, T, gname, dname, wrow, prenormed=False):
        if not prenormed:
            norm_T(subs, wrow, xnT, "xnT")
        P.tag = 'ffn_gu'
        xk = [("xnT", s) for (s, _, _) in subs]
        groups = [(0, 3), (3, 6), (6, 9), (9, 11)]
        cnt = 0
        for (p0, p1) in groups:
            j0 = 2 * p0
            nj = 2 * (p1 - p0)
            P.tag = 'ffn_gu'
            dma("sp", wd[:, 0:nj, :], scr[dname][j0:j0 + nj].rearrange("j p f -> p j f"),
                [("scr", dname, (j0 // 6) * 6)], ["wd"])
            for pi in range(p0, p1):
                slot, skey = ring_get(gname, pi)
                for jj in range(2):
                    jl = 2 * (pi - p0) + jj
                    bg = cnt % 2
                    bu = 2 + cnt % 2
                    cnt += 1
                    for kc in range(8):
                        mm(ps[bg][:, 0:T], slot[:, kc, jj * 128:(jj + 1) * 128], xnT[:, kc, 0:T], kc == 0, kc == 7,
                           [skey] + xk, [PS(bg)])
                    for kc in range(8):
                        mm(ps[bu][:, 0:T], slot[:, kc, 256 + jj * 128:256 + (jj + 1) * 128], xnT[:, kc, 0:T],
                           kc == 0, kc == 7, [skey] + xk, [PS(bu)])
                    act(sg[:, 0:T], ps[bg][:, 0:T], AF.Silu, [PS(bg)], ["t1"])
                    tt("dve", actT[:, jl, 0:T], ps[bu][:, 0:T], sg[:, 0:T], ALU.mult, [PS(bu), "t1"], [("actT", jl)])
                ring_release()
            ak = [("actT", jl) for jl in range(nj)]
            P.tag = 'ffn_dn'
            for (s, off, n) in subs:
                for hf in range(2):
                    b = 4 + dcnt["n"] % 4
                    dcnt["n"] += 1
                    for jl in range(nj):
                        mm(ps[b][:n, :], actT[:, jl, off:off + n], wd[:, jl, hf * 512:(hf + 1) * 512],
                           jl == 0, jl == nj - 1, ak + ["wd"], [PS(b)])
                    stt(h[:n, s, hf * 512:(hf + 1) * 512], ps[b][:n, :], 0.5, h[:n, s, hf * 512:(hf + 1) * 512],
                        ALU.mult, ALU.add, [PS(b), ("h", s)], [("h", s)])

    def featmajor_chunk(slot, skey, cc, T, uk, b):
        for kc in range(8):
            mm(ps[b][:, 0:T], slot[:, kc, cc * 128:(cc + 1) * 128], xnT[:, kc, 0:T], kc == 0, kc == 7,
               [skey] + uk, [PS(b)])

    def mixer(subs, T, meta):
        NS = len(subs)
        norm_T(subs, 1, xnT, "xnT")
        uk = [("xnT", s) for (s, _, _) in subs]
        sbe = "dve"
        if not meta:
            fence(["wd"] + [("actT", jl) for jl in range(6)], ALIAS_KEYS)
        P.tag = 'dt'
        for (s, off, n) in subs:
            bd = 4 + s % 4
            for kc in range(8):
                mm(ps[bd][:n, 0:16], xnT[:, kc, off:off + n], wdt[:, kc, :], kc == 0, kc == 7, uk + ["wdt"], [PS(bd)])
            tt("dve", dtmp[:n, :], ps[bd][:n, 0:16], dtb_bc[:n, :], ALU.add, [PS(bd), "dtb_bc"], ["dtmp"])
            act(dtmp[:n, :], dtmp[:n, :], AF.Exp, ["dtmp"], ["dtmp"])
            act(dtt[:n, s, :], dtmp[:n, :], AF.Ln, ["dtmp", "one_t"], [("dtt", s)], bias=one_t[:n, 0:1])
            tt("dve", att_a[:n, s, :], dtt[:n, s, :], A_bc[:n, :], ALU.mult, [("dtt", s), "A_bc"], [("att_a", s)])
        bcs = {"bc": 0, "tmb": 0}

        pend = {"p": None}

        def conv_flush():
            if pend["p"] is not None:
                pend["p"]()
                pend["p"] = None

        def conv_panel(pi):
            P.tag = 'conv'
            slot, skey = ring_get("win", pi)
            for cc in range(4):
                ch = pi * 4 + cc
                b = bcs["bc"] % 4
                bcs["bc"] += 1
                if ch < 8:
                    dst = xsT[:, ch, 0:T]; dk = [("xsT", s) for (s, _, _) in subs]
                elif ch < 12:
                    dst = BT[:, ch - 8, 0:T]; dk = [("BT", s) for (s, _, _) in subs]
                else:
                    dst = CT[:, ch - 12, 0:T]; dk = [("CT", s) for (s, _, _) in subs]
                featmajor_chunk(slot, skey, cc, T, uk, b)
                if meta:
                    xp, ca, xpk, cak = xpad, cacc, "xpad", "cacc"
                    cp(sbe, xp[:, 0:3], ctail[:, ch, :], [("ctail", ch)], [xpk])
                    cp("act", xp[:, 3:3 + T], ps[b][:, 0:T], [PS(b), xpk], [xpk])
                    cp(sbe, ctail[:, ch, :], xp[:, T:T + 3], [xpk], [("ctail", ch)])
                    ts("dve", ca[:, 0:T], xp[:, 0:T], PT[:, ch, 0:1], PT[:, ch, 4:5], ALU.mult, ALU.add,
                       [xpk, "PT"], [cak])
                    for k in range(1, 4):
                        stt(ca[:, 0:T], xp[:, k:k + T], PT[:, ch, k:k + 1], ca[:, 0:T], ALU.mult, ALU.add,
                            [xpk, cak, "PT"], [cak])
                    act(dst, ca[:, 0:T], AF.Silu, [cak], dk)
                    continue
                k2 = ch % 2
                xpb = xpbs[k2]
                xpk = "xpb%d" % k2
                dgk = ("dg", k2)
                cp("dve", xpb[:, 0:3], ctail[:, ch, :], [("ctail", ch)], [xpk])
                cp("act", xpb[:, 3:3 + T], ps[b][:, 0:T], [PS(b), xpk], [xpk])
                cp("dve", ctail[:, ch, :], xpb[:, T:T + 3], [xpk], [("ctail", ch)])
                for j in range(4):
                    ts("dve", dgs[k2][j], identb[:, :], PT[:, ch, j:j + 1], None, ALU.mult, None,
                       ["identb", "PT"], [dgk])
                conv_flush()

                def stage_b(ch=ch, k2=k2, xpb=xpb, xpk=xpk, dgk=dgk, dst=dst, dk=dk):
                    b2 = 4 + ch % 2
                    for j in range(4):
                        mm(ps[b2][:, 0:T], dgs[k2][j], xpb[:, j:j + T], j == 0, j == 3, [dgk, xpk], [PS(b2)])
                    act(dst, ps[b2][:, 0:T], AF.Silu, [PS(b2), "PT"], dk, bias=PT[:, ch, 4:5])
                pend["p"] = stage_b
            ring_release()
            if pi == 3:
                conv_flush()

        def q_panel(pi):
            P.tag = 'q'
            slot, skey = ring_get("win", 4 + pi)
            for cc in range(4):
                hd = pi * 4 + cc
                b = bcs["bc"] % 4
                bcs["bc"] += 1
                tq, tqk = (t1, "t1") if hd % 2 == 0 else (t1b, "t1b")
                featmajor_chunk(slot, skey, cc, T, uk, b)
                act(tq[:, 0:T], ps[b][:, 0:T], AF.Silu, [PS(b)], [tqk])
                qk = [("qT", s) for (s, _, _) in subs]
                tt("dve", qT[:, hd, 0:T], tq[:, 0:T], qT[:, hd, 0:T], ALU.mult, [tqk] + qk, qk)
            ring_release()

        def f_head(hd, slot, skey, cc, tset):
            (a, ak), (b2, bk), (c3, ck_) = tset
            b = bcs["bc"] % 4
            bcs["bc"] += 1
            n0 = subs[0][2]
            featmajor_chunk(slot, skey, cc, T, uk, b)
            act(a[:, 0:T], ps[b][:, 0:T], AF.Sigmoid, [PS(b)], [ak])
            yield
            ts("dve", b2[:, 0:T], a[:, 0:T], lbp[:, hd, 1:2], lbp[:, hd, 0:1], ALU.mult, ALU.add,
               [ak, "lbp"], [bk])
            yield
            act(c3[:, 0:T], b2[:, 0:T], AF.Ln, [bk], [ck_])
            ts("dve", a[:, 0:T], a[:, 0:T], lbp[:, hd, 2:3], lbp[:, hd, 1:2], ALU.mult, ALU.add,
               [ak, "lbp"], [ak])
            yield
            for (s, off, n) in subs:
                P.op("dve", lambda e, off=off, n=n: e.tensor_tensor_scan(
                    b2[:, off:off + n], onesf[:, 0:n], c3[:, off:off + n], 0.0, ALU.mult, ALU.add),
                    [ck_, "onesf"], [bk])
            b2v = b2[:, 0:T].rearrange("p (s t) -> p s t", t=n0)
            c3v = c3[:, 0:T].rearrange("p (s t) -> p s t", t=n0)
            cp("dve", gr[:, hd, 0:NS, 0:1], b2v[:, :, n0 // 2 - 1:n0 // 2], [bk], [("gr", hd)])
            cp("dve", gr[:, hd, 0:NS, 1:2], b2v[:, :, n0 - 1:n0], [bk], [("gr", hd)])
            tt("dve", c3v, b2v, gr[:, hd, 0:NS, 0:1].broadcast_to([128, NS, n0]), ALU.subtract,
               [bk, ("gr", hd)], [ck_])
            yield
            act(b2[:, 0:T], c3[:, 0:T], AF.Exp, [ck_], [bk], scale=-1.0)
            if not meta:
                act(qT[:, hd, 0:T], c3[:, 0:T], AF.Exp, [ck_], [("qT", s) for (s, _, _) in subs])
            yield
            tt("dve", kT[:, hd, 0:T], a[:, 0:T], b2[:, 0:T], ALU.mult, [ak, bk],
               [("kT", s) for (s, _, _) in subs])
            yield

        def f_panel(pi):
            P.tag = 'f'
            slot, skey = ring_get("win", 6 + pi)
            setA = ((t1, "t1"), (t2, "t2"), (t3, "t3"))
            setB = setA if meta else ((t1b, "t1b"), (t2b, "t2b"), (t3b, "t3b"))
            for pair in range(2):
                gens = [f_head(pi * 4 + pair * 2 + k, slot, skey, pair * 2 + k, (setA, setB)[k]) for k in range(2)]
                if meta:
                    for g_ in gens:
                        for _ in g_:
                            pass
                else:
                    alive = [True, True]
                    while any(alive):
                        for gi in range(2):
                            if alive[gi]:
                                try:
                                    next(gens[gi])
                                except StopIteration:
                                    alive[gi] = False
            ring_release()
            if pi == 1:
                grk = [("gr", hd) for hd in range(8)]
                tt("dve", gsc[:, :, 0:NS, 3], gr[:, :, 0:NS, 1], gr[:, :, 0:NS, 0], ALU.subtract, grk, ["gsc"])
                act(gsc[:, :, 0:NS, 0], gr[:, :, 0:NS, 0], AF.Exp, grk, ["gsc"])
                act(gsc[:, :, 0:NS, 1], gr[:, :, 0:NS, 1], AF.Exp, grk, ["gsc"])
                act(gsc[:, :, 0:NS, 2], gsc[:, :, 0:NS, 3], AF.Exp, ["gsc"], ["gsc"])

        def tm_panel(pname, pi):
            P.tag = 'izg'
            dstt, fn = {"i": (vtm, None), "z": (zs, AF.Silu), "g": (gs, AF.Silu)}[pname]
            base = {"i": 8, "z": 10, "g": 12}[pname]
            dkn = {"i": "vtm", "z": "zs", "g": "gs"}[pname]
            slot, skey = ring_get("win", base + pi)
            for (s, off, n) in subs:
                b = 4 + bcs["tmb"] % 2
                bcs["tmb"] += 1
                for kc in range(8):
                    mm(ps[b][:n, :], xnT[:, kc, off:off + n], slot[:, kc, :], kc == 0, kc == 7,
                       [skey] + uk, [PS(b)])
                if fn is None:
                    cp("act", dstt[:n, s, pi * 512:(pi + 1) * 512], ps[b][:n, :], [PS(b)], [(dkn, s)])
                else:
                    act(dstt[:n, s, pi * 512:(pi + 1) * 512], ps[b][:n, :], fn, [PS(b)], [(dkn, s)])
            ring_release()

        for (kind, arg) in mixer_sched(meta):
            if kind == "conv":
                conv_panel(arg)
            elif kind == "q":
                q_panel(arg)
            elif kind == "f":
                f_panel(arg)
            else:
                tm_panel(kind, arg)
        if not meta:
            fence(ALIAS_KEYS, ["wd"])
        bc = bcs["bc"]
        tmb = bcs["tmb"]
        ck("m5")
        for (s, off, n) in subs:
            gens = [hgrn_chunk(s, off, n, meta, sbe), ssd_chunk(s, off, n, meta, sbe)]
            tags = ['hgrn', 'ssd']
            alive = [True, True]
            while any(alive):
                for gi in range(2):
                    if alive[gi]:
                        P.tag = tags[gi]
                        try:
                            next(gens[gi])
                        except StopIteration:
                            alive[gi] = False
        if meta:
            return
        P.tag = 'gates'
        zsv = zs[:, :, :].rearrange("p s f -> p (s f)").rearrange("p (c t) -> p c t", c=8)
        gsv = gs[:, :, :].rearrange("p s f -> p (s f)").rearrange("p (c t) -> p c t", c=8)
        allz = [("zs", s) for (s, _, _) in subs]
        allg = [("gs", s) for (s, _, _) in subs]
        for (gi, view, keys) in ((0, zsv, allz), (1, gsv, allg)):
            for pi in range(2):
                slot, skey = ring_get("win", 14 + 2 * gi + pi)
                for cc in range(4):
                    b = bc % 4
                    bc += 1
                    featmajor_chunk(slot, skey, cc, T, uk, b)
                    act(view[:, pi * 4 + cc, 0:T], ps[b][:, 0:T], AF.Sigmoid, [PS(b)], keys)
                ring_release()
        ck("m8")
        P.tag = 'branch'
        yak = [("xsT", s) for (s, _, _) in subs]
        ybk = [("qT", s) for (s, _, _) in subs]
        mk = [("kT", s) for (s, _, _) in subs]
        for pi in range(2):
            sa, ska = ring_get("wa", pi)
            sb_, skb = ring_get("wb", pi)
            for cc in range(4):
                c = pi * 4 + cc
                for kc in range(8):
                    mm(ps[0][:, 0:T], sa[:, kc, cc * 128:(cc + 1) * 128], xsT[:, kc, 0:T], kc == 0, kc == 7,
                       [ska] + yak, [PS(0)])
                for kc in range(8):
                    mm(ps[1][:, 0:T], sb_[:, kc, cc * 128:(cc + 1) * 128], qT[:, kc, 0:T], kc == 0, kc == 7,
                       [skb] + ybk, [PS(1)])
                tt("dve", t1[:, 0:T], ps[0][:, 0:T], zsv[:, c, 0:T], ALU.mult, [PS(0)] + allz, ["t1"])
                tt("dve", t2[:, 0:T], ps[1][:, 0:T], gsv[:, c, 0:T], ALU.mult, [PS(1)] + allg, ["t2"])
                tt("dve", kT[:, c, 0:T], t1[:, 0:T], t2[:, 0:T], ALU.add, ["t1", "t2"], mk)
            ring_release()
            ring_release()
        ck("m9")
        P.tag = 'outp'
        for pi in range(2):
            slot, skey = ring_get("wo", pi)
            for (s, off, n) in subs:
                b = 4 + tmb % 2
                tmb += 1
                for kc in range(8):
                    mm(ps[b][:n, :], kT[:, kc, off:off + n], slot[:, kc, :], kc == 0, kc == 7, [skey] + mk, [PS(b)])
                tt("dve", h[:n, s, pi * 512:(pi + 1) * 512], ps[b][:n, :], h[:n, s, pi * 512:(pi + 1) * 512],
                   ALU.add, [PS(b), ("h", s)], [("h", s)])
            ring_release()


    def rms_groups(src, n, ngrp, gsz, skey):
        c0 = ncall["n"] % 64
        ncall["n"] += ngrp
        if c0 + ngrp > 64:
            c0 = 0
            ncall["n"] = ngrp
        for g in range(ngrp):
            act(junk[:n, 0:gsz], src[:n, g * gsz:(g + 1) * gsz], AF.Square, [skey], ["junk", ("ss", c0 + g)],
                accum_out=ss[:n, c0 + g:c0 + g + 1])
        rk = [("ss", c0 + g) for g in range(ngrp)]
        rr = [("rs", c0 + g) for g in range(ngrp)]
        act(rs[:n, c0:c0 + ngrp], ss[:n, c0:c0 + ngrp], AF.Sqrt, rk + ["eps_t"], rr, scale=1.0 / gsz, bias=eps_t[:n, 0:1])
        P.op("dve", lambda e: e.reciprocal(rs[:n, c0:c0 + ngrp], rs[:n, c0:c0 + ngrp]), rr, rr)
        return c0

    def ssd_chunk(s, off, n, meta, sbe):
        sl_ = slice(off, off + n)
        for half in range(2):
            b = half
            for cc in range(4):
                c = half * 4 + cc
                mm(ps[b][:n, cc * 128:(cc + 1) * 128], xsT[:, c, sl_], identb[:, :], True, True,
                   [("xsT", s), "identb"], [PS(b)])
            if not meta:
                cp("act", xstm[:n, half * 512:(half + 1) * 512], ps[b][:n, :], [PS(b)], ["xstm"])
            tt("dve", X[:n, half * 512:(half + 1) * 512].rearrange("p (h q) -> p h q", q=64),
               ps[b][:n, :].rearrange("p (h q) -> p h q", q=64),
               dtt[:n, s, half * 8:half * 8 + 8].unsqueeze(2).broadcast_to([n, 8, 64]), ALU.mult,
               [PS(b), ("dtt", s)], ["X"])
        yield
        mm(ps[4][:n, 0:16], trif[:n, :n], att_a[:n, s, :], True, True, ["trif", ("att_a", s)], [PS(4)])
        mm(ps[4][:, 16:32], onesf[:n, :], att_a[:n, s, :], True, True, ["onesf", ("att_a", s)], [PS(4)])
        act(ecs[:n, :], ps[4][:n, 0:16], AF.Exp, [PS(4)], ["ecs"])
        act(cdb[:, :], ps[4][:, 16:32], AF.Exp, [PS(4)], ["cdb"])
        for g in range(4):
            mm(ps[4][:n, g * 128:g * 128 + n], BT[:, g, sl_], CT[:, g, sl_], True, True,
               [("BT", s), ("CT", s)], [PS(4)])
        tt("dve", CBm[:n, :, 0:n], ps[4][:n, :].rearrange("p (g l) -> p g l", g=4)[:, :, 0:n],
           trif[:n, 0:n].unsqueeze(1).broadcast_to([n, 4, n]), ALU.mult, [PS(4), "trif"], ["CBm"])
        yield
        for half in range(2):
            tt(sbe, R[:n, :, 0:n], att_a[:n, s, half * 8:half * 8 + 8].unsqueeze(2).broadcast_to([n, 8, n]),
               trif[:n, 0:n].unsqueeze(1).broadcast_to([n, 8, n]), ALU.mult, [("att_a", s), "trif"], ["R"])
            for q4 in range(2):
                b = 2 + q4
                for hh in range(4):
                    mm(ps[b][:n, hh * 128:hh * 128 + n], slf[:n, :n], R[:n, q4 * 4 + hh, 0:n], True, True,
                       ["slf", "R"], [PS(b)])
                act(L[:n, q4 * 4:q4 * 4 + 4, 0:n], ps[b][:n, :].rearrange("p (h l) -> p h l", h=4)[:, :, 0:n],
                    AF.Exp, [PS(b)], ["L"])
            if not meta:
                tt(sbe, Mt[:n, half * 8:half * 8 + 8, 0:n].rearrange("p (g r) l -> p g r l", g=2),
                   L[:n, :, 0:n].rearrange("p (g r) l -> p g r l", g=2),
                   CBm[:n, half * 2:half * 2 + 2, 0:n].unsqueeze(2).broadcast_to([n, 2, 4, n]), ALU.mult,
                   ["L", "CBm"], [("Mt", half)])
            tt(sbe, Xd[:n, half * 512:(half + 1) * 512].rearrange("p (h q) -> p h q", q=64),
               X[:n, half * 512:(half + 1) * 512].rearrange("p (h q) -> p h q", q=64),
               L[:n, :, n - 1:n].broadcast_to([n, 8, 64]), ALU.mult, ["L", "X"], ["Xd"])
            yield
        for g in range(4):
            mm(ps[4][:n, g * 128:(g + 1) * 128], BT[:, g, sl_], identb[:, :], True, True, [("BT", s), "identb"], [PS(4)])
        cp("act", Btm[:n, :], ps[4][:n, :], [PS(4)], ["Btm"])
        yield
        if not meta:
            for hh in range(16):
                b = hh // 8
                mm(ps[b][:n, (hh % 8) * 64:(hh % 8) * 64 + 64], Mt[:n, hh, 0:n], X[:n, hh * 64:(hh + 1) * 64],
                   True, True, [("Mt", hh // 8), "X"], [PS(b)])
            for g in range(4):
                b = 2 + g // 2
                mm(ps[b][:n, (g % 2) * 256:(g % 2) * 256 + 256], CT[:, g, sl_], ssdSbf[:, g * 256:(g + 1) * 256],
                   True, True, [("CT", s), "ssdSbf"], [PS(b)])
            yield
            for half in range(2):
                hs = slice(half * 512, (half + 1) * 512)
                tt("dve", y1[:n, hs].rearrange("p (h q) -> p h q", q=64),
                   ps[2 + half][:n, :].rearrange("p (h q) -> p h q", q=64),
                   ecs[:n, half * 8:half * 8 + 8].unsqueeze(2).broadcast_to([n, 8, 64]), ALU.mult,
                   [PS(2 + half), "ecs"], ["y1"])
            tt("dve", y2[:n, :].rearrange("p (h q) -> p h q", q=64),
               xstm[:n, :].rearrange("p (h q) -> p h q", q=64),
               D_bc[:n, :].unsqueeze(2).broadcast_to([n, 16, 64]), ALU.mult, ["xstm", "D_bc"], ["y2"])
            yield
        for g in range(4):
            b = 2 + g // 2
            mm(ps[b][:, (g % 2) * 256:(g % 2) * 256 + 256], Btm[:n, g * 128:(g + 1) * 128], Xd[:n, g * 256:(g + 1) * 256],
               True, True, ["Btm", "Xd"], [PS(b)])
        tt(sbe, ssdS[:, :].rearrange("p (h q) -> p h q", q=64), ssdS[:, :].rearrange("p (h q) -> p h q", q=64),
           cdb[:, :].unsqueeze(2).broadcast_to([128, 16, 64]), ALU.mult, ["ssdS", "cdb"], ["ssdS"])
        yield
        for half in range(2):
            hs = slice(half * 512, (half + 1) * 512)
            tt("dve", ssdS[:, hs], ps[2 + half][:, :], ssdS[:, hs], ALU.add, [PS(2 + half), "ssdS"], ["ssdS"])
        cp("act", ssdSbf[:, :], ssdS[:, :], ["ssdS"], ["ssdSbf"])
        yield
        if not meta:
            tt("dve", y1[:n, :], y1[:n, :], y2[:n, :], ALU.add, ["y1", "y2"], ["y1"])
            for half in range(2):
                hs = slice(half * 512, (half + 1) * 512)
                tt("dve", y1[:n, hs], ps[half][:n, :], y1[:n, hs], ALU.add, [PS(half), "y1"], ["y1"])
            yield
            tt("dve", y1[:n, :], y1[:n, :], zs[:n, s, :], ALU.mult, ["y1", ("zs", s)], ["y1"])
            c0 = rms_groups(y1, n, 4, 256, "y1")
            yield
            tt("dve", ytm[:n, :].rearrange("p (g q) -> p g q", q=256), y1[:n, :].rearrange("p (g q) -> p g q", q=256),
               rs[:n, c0:c0 + 4].unsqueeze(2).broadcast_to([n, 4, 256]), ALU.mult,
               ["y1"] + [("rs", c0 + g) for g in range(4)], ["xn_tm"])
            transpose_tm(ytm, n, "xn_tm", xsT, "xsT", s, off, 3, banks=(0, 1))
            yield

    def hgrn_chunk(s, off, n, meta, sbe):
        sl_ = slice(off, off + n)
        ytm2 = attm[:, :, :].rearrange("p h l -> p (h l)")
        if not meta:
            for hd in range(8):
                b = 5 + hd // 4
                mm(ps[b][:n, (hd % 4) * 128:(hd % 4) * 128 + n], kT[:, hd, sl_], qT[:, hd, sl_], True, True,
                   [("kT", s), ("qT", s)], [PS(b)])
            for half in range(2):
                tt("dve", attm[:n, half * 4:half * 4 + 4, 0:n],
                   ps[5 + half][:n, :].rearrange("p (g l) -> p g l", g=4)[:, :, 0:n],
                   trif[:n, 0:n].unsqueeze(1).broadcast_to([n, 4, n]), ALU.mult, [PS(5 + half), "trif"], ["attm"])
            yield
        for half in range(2):
            for h4 in range(4):
                hd = half * 4 + h4
                mm(ps[7][:n, h4 * 128:h4 * 128 + 128], kT[:, hd, sl_], identb[:, :], True, True,
                   [("kT", s), "identb"], [PS(7)])
            cp("act", ktm[:n, half * 512:(half + 1) * 512], ps[7][:n, :], [PS(7)], ["ktm"])
        yield
        if not meta:
            tt(sbe, Sbf[:, :].rearrange("p (h v) -> p h v", v=128), hgS[:, :].rearrange("p (h v) -> p h v", v=128),
               gsc[:, :, s, 0:1].broadcast_to([128, 8, 128]), ALU.mult, ["hgS", "gsc"], ["Sbf"])
            yield
            for hd in range(8):
                b = 5 + hd // 4
                o_ = ps[b][:n, (hd % 4) * 128:(hd % 4) * 128 + 128]
                mm(o_, qT[:, hd, sl_], Sbf[:, hd * 128:(hd + 1) * 128], True, False, [("qT", s), "Sbf"], [PS(b)])
                mm(o_, attm[:n, hd, 0:n], vtm[:n, s, hd * 128:(hd + 1) * 128], False, True,
                   ["attm", ("vtm", s)], [PS(b)])
            yield
            for half in range(2):
                cp("act", y3[:n, half * 512:(half + 1) * 512], ps[5 + half][:n, :], [PS(5 + half)], ["y3"])
            c0 = rms_groups(y3, n, 8, 128, "y3")
            yield
        tt(sbe, hgS[:, :].rearrange("p (h v) -> p h v", v=128), hgS[:, :].rearrange("p (h v) -> p h v", v=128),
           gsc[:, :, s, 1:2].broadcast_to([128, 8, 128]), ALU.mult, ["hgS", "gsc"], ["hgS"])
        yield
        for half in range(2):
            for h4 in range(4):
                hd = half * 4 + h4
                mm(ps[7][:, h4 * 128:h4 * 128 + 128], ktm[:n, hd * 128:(hd + 1) * 128],
                   vtm[:n, s, hd * 128:(hd + 1) * 128], True, True, ["ktm", ("vtm", s)], [PS(7)])
            for h4 in range(4):
                hd = half * 4 + h4
                stt(hgS[:, hd * 128:(hd + 1) * 128], ps[7][:, h4 * 128:h4 * 128 + 128], gsc[:, hd, s, 2:3],
                    hgS[:, hd * 128:(hd + 1) * 128], ALU.mult, ALU.add, [PS(7), "hgS", "gsc"], ["hgS"])
            yield
        if not meta:
            tt("dve", y3[:n, :].rearrange("p (g q) -> p g q", q=128), y3[:n, :].rearrange("p (g q) -> p g q", q=128),
               rs[:n, c0:c0 + 8].unsqueeze(2).broadcast_to([n, 8, 128]), ALU.mult,
               ["y3"] + [("rs", c0 + g) for g in range(8)], ["y3"])
            yield
            tt(sbe, ytm2[:n, :], y3[:n, :], gs[:n, s, :], ALU.mult, ["y3", ("gs", s)], ["attm"])
            yield
            transpose_tm(ytm2, n, "attm", qT, "qT", s, off, 4, banks=(5, 6))
            yield

    msubs = [(0, 0, NMETA)]
    dma("sp", h[:NMETA, 0, :], meta_d[:, :], (), [("h", 0)])
    if stop == 2:
        P.emit(); return nc
    ffn(msubs, NMETA, "gu1", "dn1", 0)
    if stop == 3:
        P.emit(); return nc
    mixer(msubs, NMETA, True)
    if stop == 4:
        P.emit(); return nc
    dma("sp", m_ssdS[:, :], ssdS[:, :], ["ssdS"], ["m_ssdS"])
    dma("sp", m_hgS[:, :], hgS[:, :], ["hgS"], ["m_hgS"])
    cp("dve", m_ctail[:, :, :], ctail[:, :, :], [("ctail", ch) for ch in range(16)], ["m_ctail"])
    subs = [(s, s * 128, 128) for s in range(4)]
    rstate["main"] = True
    for q in range(nseq):
        dma("sp", ssdS[:, :], m_ssdS[:, :], ["m_ssdS"], ["ssdS"])
        cp("dve", ssdSbf[:, :], ssdS[:, :], ["ssdS"], ["ssdSbf"])
        dma("sp", hgS[:, :], m_hgS[:, :], ["m_hgS"], ["hgS"])
        cp("dve", ctail[:, :, :], m_ctail[:, :, :], ["m_ctail"], [("ctail", ch) for ch in range(16)])
        for ti in range(ntile):
            t0 = ti * TT
            first = (q == 0 and ti == 0)
            if first:
                for s4 in range(4):
                    dma("sp", h[:, s4, :], x_d[q, t0 + s4 * 128:t0 + (s4 + 1) * 128, :], (), [("h", s4)])
                norm_T(subs, 0, xnT, "xnT")
            ffn(subs, TT, "gu1", "dn1", 0, prenormed=True)
            if stop == 5:
                P.emit(); return nc
            try:
                mixer(subs, TT, False)
            except StopBuild:
                P.emit(); return nc
            if stop == 6:
                P.emit(); return nc
            ffn(subs, TT, "gu2", "dn2", 2)
            if stop == 7:
                P.emit(); return nc
            if ti + 1 < ntile:
                nxt = (q, (ti + 1) * TT)
            elif q + 1 < nseq:
                nxt = (q + 1, 0)
            else:
                nxt = None
            for (s, off, n) in subs:
                P.tag = 'final'
                c = ncall["n"] % 64
                ncall["n"] += 1
                yo = y1 if s % 2 == 0 else y2
                yk = "y1" if s % 2 == 0 else "y2"
                act(junk[:n, :], h[:n, s, :], AF.Square, [("h", s)], ["junk", ("ss", c)], accum_out=ss[:n, c:c + 1])
                act(rs[:n, c:c + 1], ss[:n, c:c + 1], AF.Sqrt, [("ss", c), "eps_t"], [("rs", c)], scale=1.0 / D,
                    bias=eps_t[:n, 0:1])
                P.op("dve", lambda e, c=c: e.reciprocal(rs[:, c:c + 1], rs[:, c:c + 1]), [("rs", c)], [("rs", c)])
                stt(yo[:, :], h[:, s, :], rs[:, c:c + 1], fnorm_bc[:, :], ALU.mult, ALU.mult,
                    [("h", s), ("rs", c), "fnorm_bc"], [yk])
                dma("sp", out_d[q, t0 + off:t0 + off + 128, :], yo[:, :], [yk], ())
                if nxt is not None:
                    nq, nt0 = nxt
                    dma("sp", h[:, s, :], x_d[nq, nt0 + off:nt0 + off + 128, :], (), [("h", s)])
            if nxt is not None:
                norm_T(subs, 0, xnT, "xnT")
    assert rstate["next_get"] == len(plan), (rstate["next_get"], len(plan))
    P.emit()
    return nc


_NC_CACHE = {}


def kernel(**inputs):
    ncores = 8
    x = np.ascontiguousarray(inputs["x"], dtype=np.float32)
    B = x.shape[0]
    per = B // ncores
    if "nc" not in _NC_CACHE:
        _NC_CACHE["nc"] = build_nc(nseq=per, ntile=4)
    nc = _NC_CACHE["nc"]
    shared = {}
    for k, v in inputs.items():
        if k == "x":
            continue
        a = np.ascontiguousarray(np.asarray(v, dtype=np.float32))
        if k == "final_norm" or k == "meta_tokens" or k == "hg_lower_bound":
            shared[k] = a
        else:
            shared[k] = np.ascontiguousarray(a[0])
    in_maps = []
    for c in range(ncores):
        m = dict(shared)
        m["x"] = np.ascontiguousarray(x[c * per:(c + 1) * per])
        in_maps.append(m)
    res = run_bass_kernel_spmd(nc, in_maps, core_ids=list(range(ncores)))
    return np.concatenate([np.asarray(r["out"]) for r in res.results], axis=0).astype(np.float32)
```
